# Optimizing a Trainium2 kernel written in Bass

```python
import math
import jax, jax.numpy as jnp
from jax import lax
import numpy as np

D_MODEL = 1024
BATCH = 2
SEQ = 8192
DEPTH = 1

SSM_GROUP = 16
N_SSM_GROUPS = 32
SSM_WIDTH = SSM_GROUP * N_SSM_GROUPS
SSM_STATE = 64
DT_MIN = 0.001
DT_MAX = 0.1
N_HEADS = 8
HEAD_DIM = 64
ATTN_WIDTH = N_HEADS * HEAD_DIM
MOBA_BLOCK = 256
MOBA_TOPK = 3
Q_CHUNK = 128
ROPE_THETA = 10000.0
D_FF = 2816
RMS_EPS = 1e-6
NEG_INF = -1e30
IN_WIDTH = SSM_WIDTH + 3 * ATTN_WIDTH + 2 * D_MODEL

kernel_name = "hybrid_s5_moba_macaron_block"


def rmsnorm(x, gain):
    xf = x.astype(jnp.float32)
    inv = lax.rsqrt(jnp.mean(xf * xf, axis=-1, keepdims=True) + RMS_EPS)
    return (xf * inv).astype(x.dtype) * gain


def swiglu(h, w_gate, w_up, w_down):
    return (jax.nn.silu(h @ w_gate) * (h @ w_up)) @ w_down


def rotary(x):
    L = x.shape[1]
    pos = jnp.arange(L, dtype=jnp.float32)
    inv_freq = ROPE_THETA ** (-jnp.arange(0, HEAD_DIM, 2, dtype=jnp.float32) / HEAD_DIM)
    ang = pos[:, None] * inv_freq[None, :]
    cos = jnp.cos(ang)[None, :, None, :].astype(x.dtype)
    sin = jnp.sin(ang)[None, :, None, :].astype(x.dtype)
    x1, x2 = jnp.split(x, 2, axis=-1)
    return jnp.concatenate([x1 * cos - x2 * sin, x2 * cos + x1 * sin], axis=-1)


def _complex_linear_combine(e1, e2):
    a1r, a1i, b1r, b1i = e1
    a2r, a2i, b2r, b2i = e2
    ar = a2r * a1r - a2i * a1i
    ai = a2r * a1i + a2i * a1r
    br = a2r * b1r - a2i * b1i + b2r
    bi = a2r * b1i + a2i * b1r + b2i
    return (ar, ai, br, bi)


def s5_mixer(u, a_re, a_im, b_re, b_im, c_re, c_im, d, log_dt):
    Bsz, L, _ = u.shape
    ug = u.astype(jnp.float32).reshape(Bsz, L, N_SSM_GROUPS, SSM_GROUP)
    dt = jnp.exp(log_dt.astype(jnp.float32))[:, None]
    ar = a_re.astype(jnp.float32)
    ai = a_im.astype(jnp.float32)
    mag = jnp.exp(ar * dt)
    abar_r = mag * jnp.cos(ai * dt)
    abar_i = mag * jnp.sin(ai * dt)
    den = ar * ar + ai * ai
    nr = abar_r - 1.0
    ni = abar_i
    fr = ((nr * ar + ni * ai) / den)[..., None]
    fi = ((ni * ar - nr * ai) / den)[..., None]
    br = b_re.astype(jnp.float32)
    bi = b_im.astype(jnp.float32)
    bbar_r = fr * br - fi * bi
    bbar_i = fr * bi + fi * br
    bu_r = jnp.einsum('blgh,gph->blgp', ug, bbar_r)
    bu_i = jnp.einsum('blgh,gph->blgp', ug, bbar_i)
    a_r = jnp.broadcast_to(abar_r, bu_r.shape)
    a_i = jnp.broadcast_to(abar_i, bu_i.shape)
    _, _, xr, xi = lax.associative_scan(_complex_linear_combine, (a_r, a_i, bu_r, bu_i), axis=1)
    y = (jnp.einsum('blgp,ghp->blgh', xr, c_re.astype(jnp.float32))
         - jnp.einsum('blgp,ghp->blgh', xi, c_im.astype(jnp.float32))
         + d.astype(jnp.float32).reshape(N_SSM_GROUPS, SSM_GROUP) * ug)
    return y.reshape(Bsz, L, SSM_WIDTH).astype(u.dtype)


def moba_attention(q, k, v):
    Bsz, H, L, Dh = q.shape
    nb = -(-L // MOBA_BLOCK)
    pad = nb * MOBA_BLOCK - L
    kp = jnp.pad(k, ((0, 0), (0, 0), (0, pad), (0, 0)))
    vp = jnp.pad(v, ((0, 0), (0, 0), (0, pad), (0, 0)))
    kb = kp.reshape(Bsz, H, nb, MOBA_BLOCK, Dh)
    vb = vp.reshape(Bsz, H, nb, MOBA_BLOCK, Dh)
    kmean = jnp.mean(kb.astype(jnp.float32), axis=3)
    ksel = min(MOBA_TOPK, nb)
    scale = HEAD_DIM ** -0.5
    bi = jnp.arange(Bsz)[:, None, None, None]
    hi = jnp.arange(H)[None, :, None, None]
    blk_ids = jnp.arange(nb)

    def chunk(c):
        q0 = c * Q_CHUNK
        qc = lax.dynamic_slice_in_dim(q, q0, Q_CHUNK, axis=2)
        cur = q0 // MOBA_BLOCK
        gate = jnp.einsum('bhqd,bhnd->bhqn', qc.astype(jnp.float32), kmean)
        gate = jnp.where(blk_ids < cur, gate, NEG_INF)
        _, idx = lax.top_k(gate, ksel)
        valid = idx < cur
        kg = kb[bi, hi, idx]
        vg = vb[bi, hi, idx]
        s_sel = jnp.einsum('bhqd,bhqskd->bhqsk', qc, kg).astype(jnp.float32) * scale
        s_sel = jnp.where(valid[..., None], s_sel, NEG_INF).reshape(Bsz, H, Q_CHUNK, ksel * MOBA_BLOCK)
        k_own = lax.dynamic_slice_in_dim(kp, cur * MOBA_BLOCK, MOBA_BLOCK, axis=2)
        v_own = lax.dynamic_slice_in_dim(vp, cur * MOBA_BLOCK, MOBA_BLOCK, axis=2)
        s_own = jnp.einsum('bhqd,bhkd->bhqk', qc, k_own).astype(jnp.float32) * scale
        qpos = q0 + jnp.arange(Q_CHUNK)
        kpos = cur * MOBA_BLOCK + jnp.arange(MOBA_BLOCK)
        s_own = jnp.where(kpos[None, :] <= qpos[:, None], s_own, NEG_INF)
        p = jax.nn.softmax(jnp.concatenate([s_sel, s_own], axis=-1), axis=-1)
        p_sel = p[..., :ksel * MOBA_BLOCK].reshape(Bsz, H, Q_CHUNK, ksel, MOBA_BLOCK).astype(v.dtype)
        p_own = p[..., ksel * MOBA_BLOCK:].astype(v.dtype)
        return (jnp.einsum('bhqsk,bhqskd->bhqd', p_sel, vg)
                + jnp.einsum('bhqk,bhkd->bhqd', p_own, v_own))

    outs = lax.map(chunk, jnp.arange(L // Q_CHUNK))
    return outs.transpose(1, 0, 3, 2, 4).reshape(Bsz, L, H * Dh)


def hybrid_mixer(h, w_in, a_re, a_im, b_re, b_im, c_re, c_im, d, log_dt,
                 glu_w, glu_b, w_branch_ssm, w_branch_attn, w_out):
    Bsz, L, _ = h.shape
    proj = h @ w_in
    offs = np.cumsum([SSM_WIDTH, ATTN_WIDTH, ATTN_WIDTH, ATTN_WIDTH, D_MODEL]).tolist()
    u, q, k, v, g_ssm, g_attn = jnp.split(proj, offs, axis=-1)
    y_ssm = jax.nn.gelu(s5_mixer(u, a_re, a_im, b_re, b_im, c_re, c_im, d, log_dt))
    y_ssm = y_ssm * jax.nn.sigmoid(y_ssm @ glu_w + glu_b)
    branch_a = y_ssm @ w_branch_ssm
    q = rotary(q.reshape(Bsz, L, N_HEADS, HEAD_DIM)).transpose(0, 2, 1, 3)
    k = rotary(k.reshape(Bsz, L, N_HEADS, HEAD_DIM)).transpose(0, 2, 1, 3)
    v = v.reshape(Bsz, L, N_HEADS, HEAD_DIM).transpose(0, 2, 1, 3)
    branch_b = moba_attention(q, k, v) @ w_branch_attn
    merged = jax.nn.sigmoid(g_ssm) * branch_a + jax.nn.sigmoid(g_attn) * branch_b
    return merged @ w_out


def setup_inputs(seed: int = 0) -> dict:
    key = jax.random.key(seed)
    ks = iter(jax.random.split(key, 40))

    def nrm(shape, scale):
        return jax.random.normal(next(ks), shape, jnp.float32) * scale

    def gain(shape):
        return 1.0 + nrm(shape, 0.01)

    G, P, Hg = N_SSM_GROUPS, SSM_STATE, SSM_GROUP
    a_im_base = jnp.pi * jnp.arange(P, dtype=jnp.float32)
    return {
        "x": nrm((BATCH, SEQ, D_MODEL), 1.0),
        "ffn1_norm": gain((DEPTH, D_MODEL)),
        "ffn1_w_gate": nrm((DEPTH, D_MODEL, D_FF), D_MODEL ** -0.5),
        "ffn1_w_up": nrm((DEPTH, D_MODEL, D_FF), D_MODEL ** -0.5),
        "ffn1_w_down": nrm((DEPTH, D_FF, D_MODEL), D_FF ** -0.5),
        "mix_norm": gain((DEPTH, D_MODEL)),
        "w_in": nrm((DEPTH, D_MODEL, IN_WIDTH), D_MODEL ** -0.5),
        "ssm_a_re": -0.5 + nrm((DEPTH, G, P), 0.01),
        "ssm_a_im": a_im_base + nrm((DEPTH, G, P), 0.01),
        "ssm_b_re": nrm((DEPTH, G, P, Hg), (2.0 * Hg) ** -0.5),
        "ssm_b_im": nrm((DEPTH, G, P, Hg), (2.0 * Hg) ** -0.5),
        "ssm_c_re": nrm((DEPTH, G, Hg, P), (2.0 * P) ** -0.5),
        "ssm_c_im": nrm((DEPTH, G, Hg, P), (2.0 * P) ** -0.5),
        "ssm_d": nrm((DEPTH, SSM_WIDTH), 1.0),
        "ssm_log_dt": jax.random.uniform(next(ks), (DEPTH, G), jnp.float32, math.log(DT_MIN), math.log(DT_MAX)),
        "glu_w": nrm((DEPTH, SSM_WIDTH, SSM_WIDTH), SSM_WIDTH ** -0.5),
        "glu_b": nrm((DEPTH, SSM_WIDTH), 0.01),
        "w_branch_ssm": nrm((DEPTH, SSM_WIDTH, D_MODEL), SSM_WIDTH ** -0.5),
        "w_branch_attn": nrm((DEPTH, ATTN_WIDTH, D_MODEL), ATTN_WIDTH ** -0.5),
        "w_out": nrm((DEPTH, D_MODEL, D_MODEL), D_MODEL ** -0.5),
        "ffn2_norm": gain((DEPTH, D_MODEL)),
        "ffn2_w_gate": nrm((DEPTH, D_MODEL, D_FF), D_MODEL ** -0.5),
        "ffn2_w_up": nrm((DEPTH, D_MODEL, D_FF), D_MODEL ** -0.5),
        "ffn2_w_down": nrm((DEPTH, D_FF, D_MODEL), D_FF ** -0.5),
        "final_norm": gain((D_MODEL,)),
    }


def reference(x, ffn1_norm, ffn1_w_gate, ffn1_w_up, ffn1_w_down, mix_norm, w_in,
              ssm_a_re, ssm_a_im, ssm_b_re, ssm_b_im, ssm_c_re, ssm_c_im, ssm_d, ssm_log_dt,
              glu_w, glu_b, w_branch_ssm, w_branch_attn, w_out,
              ffn2_norm, ffn2_w_gate, ffn2_w_up, ffn2_w_down, final_norm):
    for l in range(DEPTH):
        h = rmsnorm(x, ffn1_norm[l])
        x = x + 0.5 * swiglu(h, ffn1_w_gate[l], ffn1_w_up[l], ffn1_w_down[l])
        h = rmsnorm(x, mix_norm[l])
        x = x + hybrid_mixer(h, w_in[l], ssm_a_re[l], ssm_a_im[l], ssm_b_re[l], ssm_b_im[l],
                             ssm_c_re[l], ssm_c_im[l], ssm_d[l], ssm_log_dt[l],
                             glu_w[l], glu_b[l], w_branch_ssm[l], w_branch_attn[l], w_out[l])
        h = rmsnorm(x, ffn2_norm[l])
        x = x + 0.5 * swiglu(h, ffn2_w_gate[l], ffn2_w_up[l], ffn2_w_down[l])
    return rmsnorm(x, final_norm)
```

```python
import numpy as np
import ml_dtypes
from contextlib import ExitStack
import concourse.bass as bass
import concourse.mybir as mybir
from concourse.bass_utils import run_bass_kernel_spmd

F32 = mybir.dt.float32
BF16 = mybir.dt.bfloat16
AF = mybir.ActivationFunctionType
ALU = mybir.AluOpType
AX = mybir.AxisListType

SEM_LIMIT = 30000
import os as _os
_DBG_NOCONS = _os.environ.get("KOPT_NOCONS", "0") == "1"
_DBG_NOINTER = _os.environ.get("KOPT_NOINTER", "1") == "1"


class Sched:
    def __init__(self, nc, es):
        self.nc = nc
        self.es = es
        self.sem_es = es
        self.eng = {'pe': nc.tensor, 'act': nc.scalar, 'dve': nc.vector, 'pool': nc.gpsimd, 'sp': nc.sync}
        self.ops = {e: [] for e in self.eng}
        self.cnt = {}
        self.epoch = {}
        self.res = {}
        self.seen = {e: {} for e in self.eng}
        self.semh = {}
        self.nsb = 0
        self.fenced = {}

    def sb(self, name, shape, dtype):
        return self.es.enter_context(self.nc.sbuf_tensor("sb_" + name, list(shape), dtype))

    def ps(self, name, shape, dtype):
        return self.es.enter_context(self.nc.psum_tensor("ps_" + name, list(shape), dtype))

    def _semkey(self, base, inc):
        ep = self.epoch.get(base, 0)
        key = (base, ep)
        if self.cnt.get(key, 0) + inc > SEM_LIMIT:
            ep += 1
            self.epoch[base] = ep
            key = (base, ep)
        return key

    def add(self, eng, fn, r=(), w=(), inc=1, semname=None):
        waits = {}

        def need(tok):
            for k, v in tok.items():
                if waits.get(k, 0) < v:
                    waits[k] = v

        for key in r:
            st = self.res.get(key)
            if st is not None:
                need(st[0])
        for key in w:
            st = self.res.get(key)
            if st is not None:
                need(st[0])
                need(st[1])
        base = semname if semname is not None else ('c', eng)
        semkey = self._semkey(base, inc)
        cnt = self.cnt.get(semkey, 0) + inc
        self.cnt[semkey] = cnt
        final = []
        for k, v in waits.items():
            if isinstance(k[0], tuple) and k[0][0] == 'd' and not _DBG_NOCONS:
                v = max(v, self.cnt.get(k, 0) - (inc if k == semkey else 0))
            if self.seen[eng].get(k, 0) >= v:
                continue
            if eng == 'pe' and k[0] == ('c', 'pe'):
                continue
            self.seen[eng][k] = v
            final.append((k, v))
        self.ops[eng].append((fn, final, semkey, inc))
        tok = {semkey: cnt}
        for key in r:
            st = self.res.setdefault(key, [{}, {}])
            if st[1].get(semkey, 0) < cnt:
                st[1][semkey] = cnt
        for key in w:
            self.res[key] = [dict(tok), {}]
        return tok

    def dma(self, eng, out, in_, r=(), w=(), ch=None, **kw):
        return self.add(eng, lambda e: e.dma_start(out=out, in_=in_, **kw), r=r, w=w, inc=16,
                        semname=('d', ch if ch is not None else eng))

    def fence(self, eng, keys):
        waits = {}
        for key in keys:
            st = self.res.get(key)
            if st is not None:
                for d in st:
                    for k, v in d.items():
                        if waits.get(k, 0) < v:
                            waits[k] = v
        self.ops[eng].append((None, list(waits.items()), None, 0))

    def _sem(self, key):
        h = self.semh.get(key)
        if h is None:
            h = self.sem_es.enter_context(self.nc.semaphore(f"s{len(self.semh)}"))
            self.semh[key] = h
        return h

    def emit(self):
        fw = []
        for k, v in self.cnt.items():
            if isinstance(k[0], tuple) and k[0][0] == 'd' and self.fenced.get(k, 0) < v:
                self.fenced[k] = v
                if self.seen['sp'].get(k, 0) < v:
                    self.seen['sp'][k] = v
                    fw.append((k, v))
        if fw:
            self.ops['sp'].append((None, fw, None, 0))
        for e, ops in self.ops.items():
            for fn, waits, semkey, inc in ops:
                for k, v in waits:
                    self._sem(k)
                if semkey is not None:
                    self._sem(semkey)
        with self.nc.Block() as block:
            self._emit_block(block)
        self.ops = {e: [] for e in self.eng}

    def _emit_block(self, block):
        deco = {'pe': block.tensor, 'act': block.scalar, 'dve': block.vector, 'pool': block.gpsimd, 'sp': block.sync}

        def mk(ename):
            ops = self.ops[ename]

            def body(e):
                for fn, waits, semkey, inc in ops:
                    for k, v in waits:
                        e.wait_ge(self.semh[k], v)
                    if fn is not None:
                        fn(e).then_inc(self.semh[semkey], inc)
            return body

        for ename in self.eng:
            if self.ops[ename]:
                deco[ename](mk(ename))


D = 1024
FF = 2816
NJ = FF // 128
TOK = 2048
NT = TOK // 128
SEQ = 8192
EPS = 1e-6


def norm_T(S, P, xt, xkey, par, dst, dkey):
    junk, ssq, rstd, hb, ptr = P['junk'][par], P['ssq'][par], P['rstd'][par], P['hb'][par], P['ptr'][par]
    kj, ks, kr, kh, kp = ('junk', par), ('ssq', par), ('rstd', par), ('hb', par), ('ptr', par)
    S.add('act', lambda e: e.activation(out=junk[:], in_=xt, func=AF.Square, accum_out=ssq[:]), r=[xkey], w=[kj, ks])
    S.add('dve', lambda e: e.tensor_scalar(out=rstd[:], in0=ssq[:], scalar1=1.0 / D, scalar2=EPS, op0=ALU.mult, op1=ALU.add), r=[ks], w=[kr])
    S.add('act', lambda e: e.activation(out=rstd[:], in_=rstd[:], func=AF.Sqrt), r=[kr], w=[kr])
    S.add('dve', lambda e: e.reciprocal(out=rstd[:], in_=rstd[:]), r=[kr], w=[kr])
    S.add('dve', lambda e: e.tensor_scalar(out=hb[:], in0=xt, scalar1=rstd[:, 0:1], scalar2=None, op0=ALU.mult), r=[xkey, kr], w=[kh])
    for k in range(8):
        S.add('pe', lambda e, k=k: e.transpose(out=ptr[:, k, :], in_=hb[:, k * 128:(k + 1) * 128], identity=P['ident'][:]), r=[kh, 'ident'], w=[kp])
    S.add('dve', lambda e: e.tensor_copy(out=dst, in_=ptr[:]), r=[kp], w=[dkey])
    return rstd, kr


def alloc_norm(S, tag):
    P = {'junk': [], 'ssq': [], 'rstd': [], 'hb': [], 'ptr': []}
    for par in range(2):
        P['junk'].append(S.sb(f"{tag}junk{par}", [128, D], BF16))
        P['ssq'].append(S.sb(f"{tag}ssq{par}", [128, 1], F32))
        P['rstd'].append(S.sb(f"{tag}rstd{par}", [128, 1], F32))
        P['hb'].append(S.sb(f"{tag}hb{par}", [128, D], BF16))
        P['ptr'].append(S.ps(f"{tag}ptr{par}", [128, 8, 128], BF16))
    return P


def ffn_core(S, arena, xr, x_src, wg, wu, wd, gain_t, epilogue, tag):
    aT = S.sb(f"{tag}aT", [128, NJ, TOK], BF16)
    hT = arena[:, 0:8 * TOK].rearrange("p (k n) -> p k n", k=8)
    wd_bf = arena[:, :].rearrange("p (j d) -> p j d", j=NJ)
    hT_keys = [(tag, 'hT', t) for t in range(NT)]
    wg_st = [S.sb(f"{tag}wgst{i}", [128, 8, 128], F32) for i in range(2)]
    wu_st = [S.sb(f"{tag}wust{i}", [128, 8, 128], F32) for i in range(2)]
    wg_bf = [S.sb(f"{tag}wgbf{i}", [128, 8, 128], BF16) for i in range(2)]
    wu_bf = [S.sb(f"{tag}wubf{i}", [128, 8, 128], BF16) for i in range(2)]
    wd_st = [S.sb(f"{tag}wdst{i}", [128, D], F32) for i in range(2)]
    sg = [S.sb(f"{tag}sg{i}", [128, 512], F32) for i in range(2)]
    xn = [S.sb(f"{tag}xn{i}", [128, D], F32) for i in range(2)]
    G = [S.ps(f"{tag}G{i}", [128, 512], F32) for i in range(2)]
    U = [S.ps(f"{tag}U{i}", [128, 512], F32) for i in range(2)]
    Y = [S.ps(f"{tag}Y{i}", [128, 512], F32) for i in range(2)]
    wgv = wg.rearrange("(k p) f -> p k f", p=128)
    wuv = wu.rearrange("(k p) f -> p k f", p=128)
    gbc = gain_t[:].unsqueeze(2).broadcast_to([128, 8, 128])
    it = 0
    for j in range(NJ):
        s = j % 2
        S.dma('sp', wg_st[s][:], wgv[:, :, j * 128:(j + 1) * 128], w=[(tag, 'wgst', s)], ch=(tag, 'wg', s))
        S.dma('sp', wu_st[s][:], wuv[:, :, j * 128:(j + 1) * 128], w=[(tag, 'wust', s)], ch=(tag, 'wu', s))
        S.add('dve', lambda e, s=s: e.tensor_tensor(out=wg_bf[s][:], in0=wg_st[s][:], in1=gbc, op=ALU.mult),
              r=[(tag, 'wgst', s), (tag, 'gain')], w=[(tag, 'wgbf', s)])
        S.add('pool', lambda e, s=s: e.tensor_tensor(out=wu_bf[s][:], in0=wu_st[s][:], in1=gbc, op=ALU.mult),
              r=[(tag, 'wust', s), (tag, 'gain')], w=[(tag, 'wubf', s)])
        for g4 in range(4):
            pb = it % 2
            it += 1
            hk = [(tag, 'hT', t) for t in range(4 * g4, 4 * g4 + 4)]
            for k in range(8):
                S.add('pe', lambda e, s=s, k=k, g4=g4, pb=pb: e.matmul(G[pb][:], lhsT=wg_bf[s][:, k, :], rhs=hT[:, k, g4 * 512:(g4 + 1) * 512], start=(k == 0), stop=(k == 7)),
                      r=[(tag, 'wgbf', s)] + hk, w=[(tag, 'G', pb)])
            for k in range(8):
                S.add('pe', lambda e, s=s, k=k, g4=g4, pb=pb: e.matmul(U[pb][:], lhsT=wu_bf[s][:, k, :], rhs=hT[:, k, g4 * 512:(g4 + 1) * 512], start=(k == 0), stop=(k == 7)),
                      r=[(tag, 'wubf', s)] + hk, w=[(tag, 'U', pb)])
            S.add('act', lambda e, pb=pb: e.activation(out=sg[pb][:], in_=G[pb][:], func=AF.Silu), r=[(tag, 'G', pb)], w=[(tag, 'sg', pb)])
            S.add('dve', lambda e, pb=pb, j=j, g4=g4: e.tensor_tensor(out=aT[:, j, g4 * 512:(g4 + 1) * 512], in0=sg[pb][:], in1=U[pb][:], op=ALU.mult),
                  r=[(tag, 'sg', pb), (tag, 'U', pb)], w=[(tag, 'aT', j, g4)])
    for j in range(NJ):
        s = j % 2
        S.dma('sp', wd_st[s][:], wd[j * 128:(j + 1) * 128, :], w=[(tag, 'wdst', s)], ch=(tag, 'wd', s))
        if j % 2 == 0:
            S.add('act', lambda e, s=s, j=j: e.activation(out=wd_bf[:, j, :], in_=wd_st[s][:], func=AF.Copy),
                  r=[(tag, 'wdst', s)], w=[(tag, 'wdbf', j)] + hT_keys)
        else:
            S.add('dve', lambda e, s=s, j=j: e.tensor_copy(out=wd_bf[:, j, :], in_=wd_st[s][:]),
                  r=[(tag, 'wdst', s)], w=[(tag, 'wdbf', j)] + hT_keys)
    it = 0
    pending = None
    for t in range(NT):
        s = t % 2
        S.dma('act', xr[s][:], x_src(t), r=[(tag, 'xsrc', t)], w=[(tag, 'xr', s)], ch=(tag, 'xr', s))
        for half in range(2):
            pb = it % 2
            it += 1
            for j in range(NJ):
                S.add('pe', lambda e, j=j, t=t, half=half, pb=pb: e.matmul(Y[pb][:], lhsT=aT[:, j, t * 128:(t + 1) * 128], rhs=wd_bf[:, j, half * 512:(half + 1) * 512], start=(j == 0), stop=(j == NJ - 1)),
                      r=[(tag, 'aT', j, t // 4), (tag, 'wdbf', j)], w=[(tag, 'Y', pb)])
            S.add('dve', lambda e, s=s, half=half, pb=pb: e.scalar_tensor_tensor(out=xn[s][:, half * 512:(half + 1) * 512], in0=Y[pb][:], scalar=0.5, in1=xr[s][:, half * 512:(half + 1) * 512], op0=ALU.mult, op1=ALU.add),
                  r=[(tag, 'Y', pb), (tag, 'xr', s)], w=[(tag, 'xn', s)])
        if pending is not None:
            epilogue(*pending)
        pending = (t, xn[s], (tag, 'xn', s), s)
    epilogue(*pending)


NEG_MASK = -960.0


def phase2_proj(S, A, hG, uT, QA, KA, VA, ksum):
    w2st = [S.sb(f"w2st{i}", [128, 768], F32) for i in range(2)]
    w2b = S.sb("w2b", [128, 8, 768], BF16)
    gm = S.sb("gmix", [128, 8], F32)
    hblk = [S.sb(f"hblk{i}", [128, 8, 512], BF16) for i in range(2)]
    cosb = [S.sb(f"cosb{i}", [64, 512], F32) for i in range(2)]
    sinb = [S.sb(f"sinb{i}", [64, 512], F32) for i in range(2)]
    t1 = [S.sb(f"rt1_{i}", [64, 512], F32) for i in range(2)]
    t2 = [S.sb(f"rt2_{i}", [64, 512], F32) for i in range(2)]
    rot = [S.sb(f"rrot{i}", [64, 512], F32) for i in range(2)]
    PU = S.ps("PU", [128, 512], F32)
    PA = [S.ps(f"PA{i}", [128, 512], F32) for i in range(2)]
    PB = [S.ps(f"PB{i}", [128, 512], F32) for i in range(2)]
    PV = [S.ps(f"PV{i}", [128, 512], F32) for i in range(2)]
    S.dma('sp', gm[:], A["mix_gain"][:, :], w=['gmix'], ch='misc')
    w2v = A["w2"].rearrange("(k p) f -> p k f", p=128)
    for k in range(8):
        s = k % 2
        S.dma('sp', w2st[s][:], w2v[:, k, :], w=[('w2st', s)], ch=('w2', s))
        S.add('dve', lambda e, s=s, k=k: e.tensor_scalar(out=w2b[:, k, :], in0=w2st[s][:], scalar1=gm[:, k:k + 1], scalar2=None, op0=ALU.mult),
              r=[('w2st', s), 'gmix'], w=[('w2b', k)])
    for hh in range(2):
        S.add('pool', lambda e, hh=hh: e.memset(VA[hh][:, :, 64 * (1 - hh):64 * (1 - hh) + 64], 1.0), w=[('VA1', hh)])
        S.dma('act', KA[hh][64:96, :], A["ind"][:, :], w=[('KAind', hh)], ch='misc2')
        S.add('pool', lambda e, hh=hh: e.memset(ksum[hh][:], 0.0), w=[('ksum', hh)])
    w2keys = [('w2b', k) for k in range(8)]
    hgv = [g.rearrange("(r k p) n -> r p k n", r=4, k=8, p=128) for g in hG]
    pp = 0
    pv = 0
    for tb in range(16):
        r, c0 = tb // 4, (tb % 4) * 512
        s = tb % 2
        sl = slice(tb * 512, (tb + 1) * 512)
        S.dma('sp', hblk[s][:], hgv[tb % 4][r][:, :, :], r=[('hG', tb % 4)], w=[('hblk', s)], ch=('hblk', s))
        S.dma('act', cosb[s][:], A["cos"][:, sl], w=[('cosb', s)], ch=('cos', s))
        S.dma('act', sinb[s][:], A["sins"][:, sl], w=[('sinb', s)], ch=('sin', s))
        for k in range(8):
            S.add('pe', lambda e, s=s, k=k: e.matmul(PU[:], lhsT=w2b[:, k, 0:128], rhs=hblk[s][:, k, :], start=(k == 0), stop=(k == 7)),
                  r=w2keys + [('hblk', s)], w=['PU'])
        S.add('act', lambda e, sl=sl: e.activation(out=uT[:, sl], in_=PU[:], func=AF.Copy), r=['PU'], w=[('uT', tb)])
        for (nm, cm, cs_, dst) in (('k', 384, 512, KA), ('q', 128, 256, QA)):
            for hh in range(2):
                p = pp % 2
                pp += 1
                for k in range(8):
                    S.add('pe', lambda e, s=s, k=k, p=p, c=cm + 64 * hh: e.matmul(PA[p][0:64, :], lhsT=w2b[:, k, c:c + 64], rhs=hblk[s][:, k, :], start=(k == 0), stop=(k == 7)),
                          r=w2keys + [('hblk', s)], w=[('PA', p)])
                for k in range(8):
                    S.add('pe', lambda e, s=s, k=k, p=p, c=cs_ + 64 * hh: e.matmul(PB[p][0:64, :], lhsT=w2b[:, k, c:c + 64], rhs=hblk[s][:, k, :], start=(k == 0), stop=(k == 7)),
                          r=w2keys + [('hblk', s)], w=[('PB', p)])
                S.add('dve', lambda e, s=s, p=p: e.tensor_tensor(out=t1[p][:], in0=PA[p][0:64, :], in1=cosb[s][:], op=ALU.mult), r=[('PA', p), ('cosb', s)], w=[('rt1', p)])
                S.add('dve', lambda e, s=s, p=p: e.tensor_tensor(out=t2[p][:], in0=PB[p][0:64, :], in1=sinb[s][:], op=ALU.mult), r=[('PB', p), ('sinb', s)], w=[('rt2', p)])
                S.add('pool', lambda e, p=p: e.tensor_tensor(out=rot[p][:], in0=t1[p][:], in1=t2[p][:], op=ALU.add), r=[('rt1', p), ('rt2', p)], w=[('rrot', p)])
                S.add('act', lambda e, p=p, hh=hh, dst=dst, sl=sl: e.activation(out=dst[hh][0:64, sl], in_=rot[p][:], func=AF.Copy), r=[('rrot', p)], w=[(nm + 'A', hh, tb)])
                if nm == 'k':
                    S.add('dve', lambda e, p=p, hh=hh, tb=tb: e.tensor_reduce(out=ksum[hh][:, 2 * tb:2 * tb + 2], in_=rot[p][:].rearrange("p (a b) -> p a b", a=2), axis=AX.X, op=ALU.add),
                          r=[('rrot', p)], w=[('ksum', hh)])
        for tt in range(4):
            kt = tb * 4 + tt
            v = pv % 2
            pv += 1
            for k in range(8):
                S.add('pe', lambda e, s=s, k=k, v=v, tt=tt: e.matmul(PV[v][:, 0:128], lhsT=hblk[s][:, k, tt * 128:(tt + 1) * 128], rhs=w2b[:, k, 640:768], start=(k == 0), stop=(k == 7)),
                      r=w2keys + [('hblk', s)], w=[('PV', v)])
            S.add('act', lambda e, v=v, kt=kt: e.activation(out=VA[0][:, kt, 0:64], in_=PV[v][:, 0:64], func=AF.Copy), r=[('PV', v)], w=[('VA', 0, kt)])
            S.add('act', lambda e, v=v, kt=kt: e.activation(out=VA[1][:, kt, 64:128], in_=PV[v][:, 64:128], func=AF.Copy), r=[('PV', v)], w=[('VA', 1, kt)])


def phase2_gate(S, A, QA, ksum):
    fut = S.sb("fut", [128, 32, 32], F32)
    negb = S.sb("negb", [128, 32, 32], F32)
    identf = S.sb("identf", [128, 128], F32)
    kmb = [S.sb(f"kmb{i}", [64, 32], BF16) for i in range(2)]
    gmt = [S.sb(f"gmt{i}", [128, 32], F32) for i in range(4)]
    m8 = [S.sb(f"m8_{i}", [128, 8], F32) for i in range(4)]
    mbw = [S.sb(f"mbw{i}", [128, 96], F32) for i in range(4)]
    PG = [S.ps(f"PG{i}", [128, 512], F32) for i in range(4)]
    PT = [S.ps(f"PT{i}", [128, 512], F32) for i in range(4)]
    S.dma('sp', fut[:], A["fut"][:, :, :], w=['fut'], ch='misc')
    S.dma('sp', negb[:], A["negb"][:, :, :], w=['negb'], ch='misc')
    S.dma('sp', identf[:], A["ident_f"][:, :], w=['identf'], ch='misc')
    for i in range(4):
        S.add('pool', lambda e, i=i: e.memset(mbw[i][:], 0.0), w=[('mbw', i)])
    for i in range(2):
        S.add('dve', lambda e, i=i: e.tensor_copy(out=kmb[i][:], in_=ksum[i][:]), r=[('ksum', i)], w=[('kmb', i)])
    NBUF = 4
    items = [(hh, c) for hh in range(2) for c in range(64)]

    def stA(k):
        hh, c = items[k]
        p = k % NBUF
        csl = slice(c * 128, (c + 1) * 128)
        S.add('pe', lambda e: e.matmul(PG[p][:, 0:32], lhsT=QA[hh][0:64, csl], rhs=kmb[hh][:, :], start=True, stop=True),
              r=[('qA', hh, c // 4), ('kmb', hh)], w=[('PG', p)])

    def stB(k):
        hh, c = items[k]
        cur = c // 2
        p = k % NBUF
        S.add('dve', lambda e: e.tensor_tensor(out=gmt[p][:], in0=PG[p][:, 0:32], in1=fut[:, cur, :], op=ALU.add), r=[('PG', p), 'fut'], w=[('gmt', p)])
        S.add('dve', lambda e: e.max(out=m8[p][:], in_=gmt[p][:]), r=[('gmt', p)], w=[('m8', p)])

    def stC(k):
        hh, c = items[k]
        cur = c // 2
        p = k % NBUF
        S.add('dve', lambda e: e.scalar_tensor_tensor(out=mbw[p][:, 64:96], in0=gmt[p][:], scalar=m8[p][:, 2:3], in1=negb[:, cur, :], op0=ALU.is_lt, op1=ALU.mult),
              r=[('gmt', p), ('m8', p), 'negb'], w=[('mbw', p)])

    def stD(k):
        p = k % NBUF
        S.add('pe', lambda e: e.transpose(out=PT[p][0:96, 0:128], in_=mbw[p][:, :], identity=identf[:]), r=[('mbw', p), 'identf'], w=[('PT', p)])

    def stE(k):
        hh, c = items[k]
        p = k % NBUF
        csl = slice(c * 128, (c + 1) * 128)
        S.add('act', lambda e: e.activation(out=QA[hh][64:96, csl], in_=PT[p][64:96, 0:128], func=AF.Copy), r=[('PT', p)], w=[('qM', hh, c)])

    n = len(items)
    for step in range(n + 3):
        for lag, st in ((0, stA), (1, stB), (2, stC), (2, stD), (3, stE)):
            k = step - lag
            if 0 <= k < n:
                st(k)


def phase2_attn(S, A, QA, KA, VA, gin2, PBc1, after_head=None, nps=4):
    tri = S.sb("tri", [128, 128], BF16)
    ones = S.sb("ones1", [128, 128], F32)
    Pt = [S.sb(f"Pt{i}", [128, 512], BF16) for i in range(nps)]
    Osb = [S.sb(f"Osb{i}", [128, 512], F32) for i in range(2)]
    Ao = [S.sb(f"Ao{i}", [128, 512], BF16) for i in range(2)]
    PS = [S.ps(f"PS{i}", [128, 512], F32) for i in range(nps)]
    PO = [S.ps(f"PO{i}", [128, 512], F32) for i in range(2)]
    S.dma('sp', tri[:], A["tri"][:, :], w=['tri'], ch='misc')
    S.add('pool', lambda e: e.memset(ones[:], 1.0), w=['ones1'])
    it = 0
    gi = 0
    for hh in range(2):
        lo, hi = (0, 64) if hh == 0 else (64, 128)
        dn = 64 if hh == 0 else 0
        for i in range(16):
            g = gi % 2
            gi += 1
            nkt = 4 * i + 4
            qkeys = [('qA', hh, i)] + [('qM', hh, c) for c in range(4 * i, 4 * i + 4)]
            def smm(kt, p):
                nq0 = max(0, kt - 4 * i)
                c0 = nq0 * 128
                qsl = slice(i * 512 + c0, (i + 1) * 512)
                ksl = slice(kt * 128, (kt + 1) * 128)
                S.add('pe', lambda e, hh=hh, p=p, c0=c0, qsl=qsl, ksl=ksl: e.matmul(PS[p][:, c0:512], lhsT=KA[hh][0:96, ksl], rhs=QA[hh][0:96, qsl], start=True, stop=True),
                      r=qkeys + [('kA', hh, kt // 4), ('KAind', hh)], w=[('PS', p)])

            ahead = nps - 2
            for a_ in range(min(ahead, nkt)):
                smm(a_, (it + a_) % nps)
            for kt in range(nkt):
                p = it % nps
                it += 1
                nq0 = max(0, kt - 4 * i)
                c0 = nq0 * 128
                if kt + ahead < nkt:
                    smm(kt + ahead, (it + ahead - 1) % nps)
                S.add('act', lambda e, p=p, c0=c0: e.activation(out=Pt[p][:, c0:512], in_=PS[p][:, c0:512], func=AF.Exp, scale=0.125), r=[('PS', p)], w=[('Pt', p)])
                if kt >= 4 * i:
                    S.add('pool', lambda e, p=p, c0=c0: e.tensor_tensor(out=Pt[p][:, c0:c0 + 128], in0=Pt[p][:, c0:c0 + 128], in1=tri[:], op=ALU.mult), r=[('Pt', p), 'tri'], w=[('Pt', p)])
                S.add('pe', lambda e, hh=hh, p=p, g=g, c0=c0, kt=kt, nkt=nkt: e.matmul(PO[g][:, c0:512], lhsT=VA[hh][:, kt, :], rhs=Pt[p][:, c0:512], start=(kt == 0), stop=(kt == nkt - 1)),
                      r=[('Pt', p), ('VA', hh, kt), ('VA1', hh)], w=[('PO', g)])
                yield
            S.add('act', lambda e, g=g: e.activation(out=Osb[g][:], in_=PO[g][:], func=AF.Copy), r=[('PO', g)], w=[('Osb', g)])
            S.add('dve', lambda e, g=g, dn=dn: e.reciprocal(out=Osb[g][dn:dn + 1, :], in_=Osb[g][dn:dn + 1, :]), r=[('Osb', g)], w=[('Osb', g)])
            S.add('pe', lambda e, g=g, dn=dn, hi=hi: e.matmul(PBc1[0:hi, :], lhsT=ones[dn:dn + 1, 0:hi], rhs=Osb[g][dn:dn + 1, :], start=True, stop=True),
                  r=[('Osb', g), 'ones1'], w=['PCbank'])
            S.add('dve', lambda e, g=g, lo=lo, hi=hi: e.tensor_tensor(out=Ao[g][lo:hi, :], in0=Osb[g][lo:hi, :], in1=PBc1[lo:hi, :], op=ALU.mult),
                  r=[('Osb', g), 'PCbank'], w=[('Ao', g)])
            S.dma('sp', gin2[2 + hh][:, i * 512:(i + 1) * 512], Ao[g][lo:hi, :], r=[('Ao', g)], w=[('gin2a', hh, i)], ch=('ao', g))
        if after_head is not None:
            after_head(hh)


TC = 256
I32 = mybir.dt.int32
TWO_PI = float(2.0 * np.pi)
GELU_K = float(2.0 * np.sqrt(2.0 / np.pi))


def phase2_ssm_prep(S, A):
    nc = S.nc
    ECS = S.sb("ECS", [128, 8, 2, TC], F32)
    Ec = ECS[:, :, 0, :]
    Es = ECS[:, :, 1, :]
    Bp1 = S.sb("Bp1", [128, 8, 128], BF16)
    Bp2 = S.sb("Bp2", [128, 8, 128], BF16)
    W1p = S.sb("W1p", [128, 8, 128], BF16)
    W2p = S.sb("W2p", [128, 8, 128], BF16)
    R5 = S.sb("R5", [128, 8, 128], F32)
    mag = S.sb("mag", [128, 8], F32)
    carry = S.sb("carry", [128, 8], F32)
    dvec = S.sb("dvec", [128, 1], F32)
    with ExitStack() as esp:
        S.es = esp
        names = ["are", "aim", "ldt", "dt", "th", "c", "s", "cc", "ss", "cs", "abr", "abi", "den", "nr", "fr", "fi",
                 "fin", "frs", "q1", "q2", "pc", "ps", "pss", "kf"]
        T = {n: S.sb("sp_" + n, [128, 8], F32) for n in names}
        ki = S.sb("sp_ki", [128, 8], I32)
        sgn = S.sb("sgn", [128, 2], F32)
        T1 = S.sb("T1", [128, 8, 16], F32)
        T2 = S.sb("T2", [128, 8, 16], F32)
        CA = S.sb("CA", [128, 8, 16], F32)
        CB = S.sb("CB", [128, 8, 16], F32)
        tmp16 = S.sb("tmp16", [128, 16], F32)
        Bw1 = S.sb("Bw1", [128, 8, 128], F32)
        Bw2 = S.sb("Bw2", [128, 8, 128], F32)
        identf = S.sb("identf2", [128, 128], F32)
        jmat = S.sb("jmat", [128, 128], F32)
        tmpR = S.sb("tmpR", [128, 128], F32)
        X1 = S.sb("X1", [128, 8, TC // 2], F32)
        X2 = S.sb("X2", [128, 8, TC // 2], F32)
        PTr = [S.ps(f"PTr{i}", [128, 512], F32) for i in range(2)]
        for nm, src in (("are", "s_are"), ("aim", "s_aim"), ("ldt", "s_ldt")):
            S.dma('sp', T[nm][:], A[src][:, :], w=[nm], ch='misc')
        S.dma('sp', sgn[:], A["sgn"][:, :], w=['sgn'], ch='misc')
        S.dma('sp', dvec[:], A["s_d"][:, :], w=['dvec'], ch='misc')
        S.dma('sp', T1[:], A["s_T1"][:, :, :], w=['T1'], ch='misc')
        S.dma('sp', T2[:], A["s_T2"][:, :, :], w=['T2'], ch='misc')
        S.dma('sp', CA[:], A["s_CA"][:, :, :], w=['CA'], ch='misc')
        S.dma('sp', CB[:], A["s_CB"][:, :, :], w=['CB'], ch='misc')
        S.dma('sp', identf[:], A["ident_f"][:, :], w=['identf2'], ch='misc')
        S.dma('sp', jmat[:], A["jmat"][:, :], w=['jmat'], ch='misc')

        def tt(o, a, b, op, eng='dve'):
            S.add(eng, lambda e: e.tensor_tensor(out=T[o][:], in0=T[a][:], in1=T[b][:], op=op), r=[a, b], w=[o])

        def tsc(o, a, s1, s2, op0, op1=None, eng='dve'):
            if op1 is None:
                S.add(eng, lambda e: e.tensor_scalar(out=T[o][:], in0=T[a][:], scalar1=s1, scalar2=None, op0=op0), r=[a], w=[o])
            else:
                S.add(eng, lambda e: e.tensor_scalar(out=T[o][:], in0=T[a][:], scalar1=s1, scalar2=s2, op0=op0, op1=op1), r=[a], w=[o])

        def act(o, a, func, scale=1.0):
            S.add('act', lambda e: e.activation(out=T[o][:], in_=T[a][:], func=func, scale=scale), r=[a], w=[o])

        act("dt", "ldt", AF.Exp)
        tt("q1", "are", "dt", ALU.mult)
        act("q2", "q1", AF.Exp)
        S.add('dve', lambda e: e.tensor_copy(out=mag[:], in_=T["q2"][:]), r=["q2"], w=["mag"])
        tt("th", "aim", "dt", ALU.mult)
        tsc("kf", "th", 1.0 / TWO_PI, None, ALU.mult)
        S.add('dve', lambda e: e.tensor_copy(out=ki[:], in_=T["kf"][:]), r=["kf"], w=["ki"])
        S.add('dve', lambda e: e.tensor_copy(out=T["kf"][:], in_=ki[:]), r=["ki"], w=["kf"])
        S.add('dve', lambda e: e.scalar_tensor_tensor(out=T["th"][:], in0=T["kf"][:], scalar=-TWO_PI, in1=T["th"][:], op0=ALU.mult, op1=ALU.add), r=["kf", "th"], w=["th"])
        tsc("q1", "th", float(np.pi), -TWO_PI, ALU.is_gt, ALU.mult)
        tt("th", "th", "q1", ALU.add)
        tsc("q1", "th", float(-np.pi), TWO_PI, ALU.is_lt, ALU.mult)
        tt("th", "th", "q1", ALU.add)
        act("s", "th", AF.Sin)
        tsc("q1", "th", -1.0, None, ALU.mult)
        tt("q1", "q1", "th", ALU.max)
        tsc("q1", "q1", -1.0, float(np.pi / 2), ALU.mult, ALU.add)
        act("c", "q1", AF.Sin)
        tt("abr", "q2", "c", ALU.mult)
        tt("abi", "q2", "s", ALU.mult)
        tt("den", "are", "are", ALU.mult)
        tt("q1", "aim", "aim", ALU.mult)
        tt("den", "den", "q1", ALU.add)
        S.add('dve', lambda e: e.reciprocal(out=T["den"][:], in_=T["den"][:]), r=["den"], w=["den"])
        tsc("nr", "abr", -1.0, None, ALU.add)
        tt("q1", "nr", "are", ALU.mult)
        tt("q2", "abi", "aim", ALU.mult)
        tt("fr", "q1", "q2", ALU.add)
        tt("fr", "fr", "den", ALU.mult)
        tt("q1", "abi", "are", ALU.mult)
        tt("q2", "nr", "aim", ALU.mult)
        tt("fi", "q1", "q2", ALU.subtract)
        tt("fi", "fi", "den", ALU.mult)
        S.add('dve', lambda e: e.tensor_scalar(out=T["fin"][:], in0=T["fi"][:], scalar1=sgn[:, 1:2], scalar2=None, op0=ALU.mult), r=["fi", "sgn"], w=["fin"])
        S.add('dve', lambda e: e.tensor_scalar(out=T["frs"][:], in0=T["fr"][:], scalar1=sgn[:, 0:1], scalar2=None, op0=ALU.mult), r=["fr", "sgn"], w=["frs"])
        S.add('pool', lambda e: e.memset(Bw1[:], 0.0), w=["Bw1"])
        S.add('pool', lambda e: e.memset(Bw2[:], 0.0), w=["Bw2"])
        S.add('pool', lambda e: e.memset(W1p[:], 0.0), w=["W1p"])
        S.add('pool', lambda e: e.memset(W2p[:], 0.0), w=["W2p"])
        S.add('pool', lambda e: e.memset(carry[:], 0.0), w=[("carry", gl) for gl in range(8)])
        for gl in range(8):
            cs_ = slice(16 * gl, 16 * gl + 16)
            S.add('dve', lambda e, gl=gl: e.tensor_scalar(out=tmp16[:], in0=T1[:, gl, :], scalar1=T["fr"][:, gl:gl + 1], scalar2=None, op0=ALU.mult), r=["T1", "fr"], w=["tmp16"])
            S.add('dve', lambda e, gl=gl, cs_=cs_: e.scalar_tensor_tensor(out=Bw1[:, gl, cs_], in0=T2[:, gl, :], scalar=T["fin"][:, gl:gl + 1], in1=tmp16[:], op0=ALU.mult, op1=ALU.add), r=["T2", "fin", "tmp16"], w=["Bw1"])
            S.add('dve', lambda e, gl=gl: e.tensor_scalar(out=tmp16[:], in0=T2[:, gl, :], scalar1=T["frs"][:, gl:gl + 1], scalar2=None, op0=ALU.mult), r=["T2", "frs"], w=["tmp16"])
            S.add('dve', lambda e, gl=gl, cs_=cs_: e.scalar_tensor_tensor(out=Bw2[:, gl, cs_], in0=T1[:, gl, :], scalar=T["fi"][:, gl:gl + 1], in1=tmp16[:], op0=ALU.mult, op1=ALU.add), r=["T1", "fi", "tmp16"], w=["Bw2"])
            S.add('dve', lambda e, gl=gl, cs_=cs_: e.tensor_scalar(out=W1p[:, gl, cs_], in0=CA[:, gl, :], scalar1=sgn[:, 0:1], scalar2=None, op0=ALU.mult), r=["CA", "sgn"], w=["W1p"])
            S.add('dve', lambda e, gl=gl, cs_=cs_: e.tensor_scalar(out=W2p[:, gl, cs_], in0=CB[:, gl, :], scalar1=-1.0, scalar2=None, op0=ALU.mult), r=["CB"], w=["W2p"])
        it = 0
        for (Bw, Bp, kw, kp) in ((Bw1, Bp1, "Bw1", "Bp1"), (Bw2, Bp2, "Bw2", "Bp2")):
            for gl in range(8):
                p = it % 2
                it += 1
                S.add('pe', lambda e, Bw=Bw, gl=gl, p=p: e.transpose(out=PTr[p][:, 0:128], in_=Bw[:, gl, :], identity=identf[:]), r=[kw, "identf2"], w=[("PTr", p)])
                S.add('act', lambda e, Bp=Bp, gl=gl, p=p: e.activation(out=Bp[:, gl, :], in_=PTr[p][:, 0:128], func=AF.Copy), r=[("PTr", p)], w=[kp])
        S.add('dve', lambda e: e.tensor_copy(out=Ec[:, :, 0:1], in_=T["c"][:].unsqueeze(2)), r=["c"], w=["Ec"])
        S.add('dve', lambda e: e.tensor_copy(out=Es[:, :, 0:1], in_=T["s"][:].unsqueeze(2)), r=["s"], w=["Es"])
        S.add('dve', lambda e: e.tensor_copy(out=T["pc"][:], in_=T["c"][:]), r=["c"], w=["pc"])
        S.add('dve', lambda e: e.tensor_copy(out=T["ps"][:], in_=T["s"][:]), r=["s"], w=["ps"])
        n = 1
        while n < TC:
            pcb = T["pc"][:].unsqueeze(2).broadcast_to([128, 8, n])
            psb = T["ps"][:].unsqueeze(2).broadcast_to([128, 8, n])
            S.add('dve', lambda e, n=n, pcb=pcb: e.tensor_tensor(out=X1[:, :, 0:n], in0=Ec[:, :, 0:n], in1=pcb, op=ALU.mult), r=["Ec", "pc"], w=["X1"])
            S.add('pool', lambda e, n=n, psb=psb: e.tensor_tensor(out=X2[:, :, 0:n], in0=Es[:, :, 0:n], in1=psb, op=ALU.mult), r=["Es", "ps"], w=["X2"])
            S.add('dve', lambda e, n=n: e.tensor_tensor(out=Ec[:, :, n:2 * n], in0=X1[:, :, 0:n], in1=X2[:, :, 0:n], op=ALU.subtract), r=["X1", "X2"], w=["Ec"])
            S.add('dve', lambda e, n=n, pcb=pcb: e.tensor_tensor(out=X1[:, :, 0:n], in0=Es[:, :, 0:n], in1=pcb, op=ALU.mult), r=["Es", "pc"], w=["X1"])
            S.add('pool', lambda e, n=n, psb=psb: e.tensor_tensor(out=X2[:, :, 0:n], in0=Ec[:, :, 0:n], in1=psb, op=ALU.mult), r=["Ec", "ps"], w=["X2"])
            S.add('dve', lambda e, n=n: e.tensor_tensor(out=Es[:, :, n:2 * n], in0=X1[:, :, 0:n], in1=X2[:, :, 0:n], op=ALU.add), r=["X1", "X2"], w=["Es"])
            tt("cc", "pc", "pc", ALU.mult)
            tt("ss", "ps", "ps", ALU.mult)
            tt("cs", "pc", "ps", ALU.mult)
            tt("pc", "cc", "ss", ALU.subtract)
            tsc("ps", "cs", 2.0, None, ALU.mult)
            n *= 2
        S.add('dve', lambda e: e.tensor_scalar(out=T["pss"][:], in0=T["ps"][:], scalar1=sgn[:, 0:1], scalar2=None, op0=ALU.mult), r=["ps", "sgn"], w=["pss"])
        for gl in range(8):
            S.add('dve', lambda e, gl=gl: e.tensor_scalar(out=tmpR[:], in0=identf[:], scalar1=T["pc"][:, gl:gl + 1], scalar2=None, op0=ALU.mult), r=["identf2", "pc"], w=["tmpR"])
            S.add('dve', lambda e, gl=gl: e.scalar_tensor_tensor(out=R5[:, gl, :], in0=jmat[:], scalar=T["pss"][:, gl:gl + 1], in1=tmpR[:], op0=ALU.mult, op1=ALU.add), r=["jmat", "pss", "tmpR"], w=["R5"])
        S.emit()
    return dict(Ec=Ec, Es=Es, ECS=ECS, Bp1=Bp1, Bp2=Bp2, W1p=W1p, W2p=W2p, R5=R5, mag=mag, carry=carry, dvec=dvec)


def phase2_ssm_main(S, A, uT, gin2, Tn, PCb, lean=False):
    Ec, Es, Bp1, Bp2, W1p, W2p, R5, mag, carry, dvec = (Tn[k] for k in ("Ec", "Es", "Bp1", "Bp2", "W1p", "W2p", "R5", "mag", "carry", "dvec"))
    ECS = Tn["ECS"]
    NS = 6
    t12 = [S.sb(f"st12_{i}", [128, 2 * TC], F32) for i in range(NS)]
    bp = [S.sb(f"sbp{i}", [128, TC], F32) for i in range(NS)]
    z = [S.sb(f"sz{i}", [128, TC], F32) for i in range(NS)]
    zcs = [S.sb(f"szcs{i}", [128, 2, TC], BF16) for i in range(NS)]
    ysb = [S.sb(f"ysb{i}", [128, TC], F32) for i in range(2)]
    yw = [S.sb(f"yw{i}", [128, TC], F32) for i in range(2)]
    yo = [S.sb(f"yo{i}", [128, TC], BF16) for i in range(2)]
    n12, ny, npc = (1, 1, 1) if lean else (3, 2, 2)
    P12l = [S.ps(f"P12_{i}", [128, 512], F32) for i in range(n12)]
    PYl = [S.ps(f"PY{i}", [128, 512], F32) for i in range(ny)]
    PCl = [PCb] + ([S.ps("PCb2", [128, 512], F32)] if npc == 2 else [])
    NB = SEQ // TC
    N = NB * 8

    def info(k):
        tb, gl = divmod(k, 8)
        return tb, gl, slice(tb * TC, (tb + 1) * TC), ('uT', (tb * TC) // 512), tb % ny

    def stage1(k):
        tb, gl, sl, ukey, y = info(k)
        p = k % NS
        P12 = P12l[k % n12]
        k12 = ("P12", k % n12)
        S.add('pe', lambda e: e.matmul(P12[:, 0:TC], lhsT=Bp1[:, gl, :], rhs=uT[:, sl], start=True, stop=True), r=["Bp1", ukey], w=[k12])
        S.add('pe', lambda e: e.matmul(P12[:, TC:2 * TC], lhsT=Bp2[:, gl, :], rhs=uT[:, sl], start=True, stop=True), r=["Bp2", ukey], w=[k12])
        S.add('dve', lambda e: e.tensor_tensor(out=t12[p][:], in0=P12[:, 0:2 * TC], in1=ECS[:, gl, :, :].rearrange("p a b -> p (a b)"), op=ALU.mult), r=[k12, "Ec", "Es"], w=[("st12", p)])

    def stage1b(k):
        p = k % NS
        S.add('dve', lambda e: e.tensor_tensor(out=bp[p][:], in0=t12[p][:, 0:TC], in1=t12[p][:, TC:2 * TC], op=ALU.add), r=[("st12", p)], w=[("sbp", p)])

    def stage2(k):
        tb, gl, sl, ukey, y = info(k)
        p = k % NS
        PCx = PCl[k % npc]
        kpc = ("PCbankS", k % npc)
        PY = PYl[y]
        kpy = ("PY", y)
        S.add('dve', lambda e: e.tensor_tensor_scan(out=z[p][:], data0=mag[:, gl:gl + 1].broadcast_to([128, TC]), data1=bp[p][:], initial=carry[:, gl:gl + 1], op0=ALU.mult, op1=ALU.add),
              r=[("sbp", p), "mag", ("carry", gl)], w=[("sz", p)])
        S.add('pe', lambda e: e.matmul(PCx[:, 0:1], lhsT=R5[:, gl, :], rhs=z[p][:, TC - 1:TC], start=True, stop=True), r=["R5", ("sz", p)], w=[kpc])

    def stage2b(k):
        tb, gl, sl, ukey, y = info(k)
        p = k % NS
        PY = PYl[y]
        kpy = ("PY", y)
        S.add('dve', lambda e: e.tensor_tensor(out=zcs[p][:], in0=z[p][:].unsqueeze(1).broadcast_to([128, 2, TC]), in1=ECS[:, gl, :, :], op=ALU.mult), r=[("sz", p), "Ec", "Es"], w=[("szcs", p)])
        S.add('pe', lambda e: e.matmul(PY[:, 0:TC], lhsT=W1p[:, gl, :], rhs=zcs[p][:, 0, :], start=(gl == 0), stop=False), r=["W1p", ("szcs", p)], w=[kpy])
        S.add('pe', lambda e: e.matmul(PY[:, 0:TC], lhsT=W2p[:, gl, :], rhs=zcs[p][:, 1, :], start=False, stop=(gl == 7)), r=["W2p", ("szcs", p)], w=[kpy])

    def stage3(k):
        tb, gl, sl, ukey, y = info(k)
        PCx = PCl[k % npc]
        kpc = ("PCbankS", k % npc)
        S.add('dve', lambda e: e.tensor_copy(out=carry[:, gl:gl + 1], in_=PCx[:, 0:1]), r=[kpc], w=[("carry", gl)])

    def epilogue(tb):
        _, _, sl, ukey, y = info(tb * 8)
        PY = PYl[y]
        yb = tb % 2
        S.add('dve', lambda e: e.scalar_tensor_tensor(out=ysb[y][:], in0=uT[:, sl], scalar=dvec[:, 0:1], in1=PY[:, 0:TC], op0=ALU.mult, op1=ALU.add), r=[ukey, 'dvec', ("PY", y)], w=[("ysb", y)])
        S.add('act', lambda e: e.activation(out=yw[y][:], in_=ysb[y][:], func=AF.Square), r=[("ysb", y)], w=[("yw", y)])
        S.add('dve', lambda e: e.tensor_scalar(out=yw[y][:], in0=yw[y][:], scalar1=0.044715, scalar2=1.0, op0=ALU.mult, op1=ALU.add), r=[("yw", y)], w=[("yw", y)])
        S.add('dve', lambda e: e.tensor_tensor(out=yw[y][:], in0=yw[y][:], in1=ysb[y][:], op=ALU.mult), r=[("yw", y), ("ysb", y)], w=[("yw", y)])
        S.add('act', lambda e: e.activation(out=yw[y][:], in_=yw[y][:], func=AF.Sigmoid, scale=GELU_K), r=[("yw", y)], w=[("yw", y)])
        S.add('dve', lambda e: e.tensor_tensor(out=yo[y][:], in0=yw[y][:], in1=ysb[y][:], op=ALU.mult), r=[("yw", y), ("ysb", y)], w=[("yo", y)])
        S.dma('sp', gin2[0][:, sl], yo[y][0:64, :], r=[("yo", y)], w=[('gin2y', tb, 0)], ch=('yo', y))
        S.dma('sp', gin2[1][:, sl], yo[y][64:128, :], r=[("yo", y)], w=[('gin2y', tb, 1)], ch=('yo', y))

    lag2 = 1 if lean else 2
    for step in range(N + lag2 + 2):
        if step < N:
            stage1(step)
        if 0 <= step - lag2 < N:
            stage2(step - lag2)
        if step < N:
            stage1b(step)
        if 0 <= step - lag2 - 1 < N:
            stage3(step - lag2 - 1)
        if 0 <= step - lag2 < N:
            stage2b(step - lag2)
            if (step - lag2) % 8 == 7:
                epilogue((step - lag2) // 8)
        yield


def phase3_mix(S, A, gin1, gout2, x1s, x2s, arena, P):
    hT = arena[:, 0:8 * TOK].rearrange("p (k n) -> p k n", k=8)
    stg = [S.sb(f"stg{i}", [128, 2048], F32) for i in range(2)]
    glw = S.sb("glw", [128, 4, 512], BF16)
    wbs = S.sb("wbs", [128, 4, D], BF16)
    wba = S.sb("wba", [128, 4, D], BF16)
    wgt = S.sb("wgt", [128, 8, 2048], BF16)
    wo = S.sb("wo", [128, 8, D], BF16)
    gm = S.sb("gmix3", [128, 8], F32)
    glb = S.sb("glb", [128, 4], F32)
    Yg = S.sb("Yg", [128, 4, 512], BF16)
    At = S.sb("At", [128, 4, 512], BF16)
    ygl = S.sb("ygl", [128, 4, 512], BF16)
    h1b = S.sb("h1b", [128, 8, 512], BF16)
    mT = S.sb("mT", [128, 8, 512], BF16)
    sgz = S.sb("sgz", [128, 512], F32)
    sgs = [S.sb(f"sgs{i}", [128, 512], F32) for i in range(2)]
    sga = [S.sb(f"sga{i}", [128, 512], F32) for i in range(2)]
    m1 = [S.sb(f"m1_{i}", [128, 512], F32) for i in range(2)]
    m2 = [S.sb(f"m2_{i}", [128, 512], F32) for i in range(2)]
    x1t = [S.sb(f"x1t{i}", [128, D], F32) for i in range(2)]
    x2t = [S.sb(f"x2t{i}", [128, D], F32) for i in range(2)]
    PX = [S.ps(f"PX{i}", [128, 512], F32) for i in range(2)]
    PAa = S.ps("PAa", [128, 512], F32)
    PBb = S.ps("PBb", [128, 512], F32)
    PGs = S.ps("PGs", [128, 512], F32)
    PGa = S.ps("PGa", [128, 512], F32)
    S.dma('sp', gm[:], A["mix_gain"][:, :], w=['gmix3'], ch='misc')
    S.dma('sp', glb[:], A["glu_b"][:, :], w=['glb'], ch='misc')
    si = [0]

    def stage(src_ap, ncols, dst_ap, key, gain_col=None, shape3=None):
        sidx = si[0] % 2
        si[0] += 1
        st = stg[sidx][:, 0:ncols]
        stv = st if shape3 is None else st.rearrange("p (a b) -> p a b", a=shape3)
        S.dma('sp', stv, src_ap, w=[('stg', sidx)], ch=('stg', sidx))
        if sidx == 0:
            if gain_col is None:
                S.add('dve', lambda e: e.tensor_copy(out=dst_ap, in_=stv), r=[('stg', sidx)], w=[key])
            else:
                S.add('dve', lambda e: e.tensor_scalar(out=dst_ap, in0=stv, scalar1=gain_col, scalar2=None, op0=ALU.mult), r=[('stg', sidx), 'gmix3'], w=[key])
        else:
            if gain_col is None:
                S.add('act', lambda e: e.activation(out=dst_ap, in_=stv, func=AF.Copy), r=[('stg', sidx)], w=[key])
            else:
                S.add('act', lambda e: e.activation(out=dst_ap, in_=stv, func=AF.Copy, scale=gain_col), r=[('stg', sidx), 'gmix3'], w=[key])

    stage(A["glu_w"].rearrange("(k p) f -> p k f", p=128), 2048, glw[:], 'glw', shape3=4)
    for kk in range(2):
        stage(A["w_bs"].rearrange("(k p) f -> p k f", p=128)[:, 2 * kk:2 * kk + 2, :], 2048, wbs[:, 2 * kk:2 * kk + 2, :], ('wbs', kk), shape3=2)
        stage(A["w_ba"].rearrange("(k p) f -> p k f", p=128)[:, 2 * kk:2 * kk + 2, :], 2048, wba[:, 2 * kk:2 * kk + 2, :], ('wba', kk), shape3=2)
    wiv = A["w_in"].rearrange("(k p) f -> p k f", p=128)
    for k in range(8):
        stage(wiv[:, k, 2048:4096], 2048, wgt[:, k, :], ('wgt', k), gain_col=gm[:, k:k + 1])
    for kk in range(4):
        stage(A["w_out"].rearrange("(k p) f -> p k f", p=128)[:, 2 * kk:2 * kk + 2, :], 2048, wo[:, 2 * kk:2 * kk + 2, :], ('wo', kk), shape3=2)
    wbs_k = [('wbs', 0), ('wbs', 1)]
    wba_k = [('wba', 0), ('wba', 1)]
    wgt_k = [('wgt', k) for k in range(8)]
    wo_k = [('wo', kk) for kk in range(4)]
    g2v = [g.rearrange("(r c) n -> c r n", r=4) for g in gout2]
    g1v = [g.rearrange("(k p) n -> p k n", p=128) for g in gin1]
    px = 0
    pend3 = []
    pidc = {}

    def base_off(e):
        if 'v' not in pidc:
            pidc['v'] = (e.partition_id() % 4) * TOK
        return pidc['v']

    for g in range(4):
        for q in range(4):
            dst = (Yg if q < 2 else At)[64 * (q % 2):64 * (q % 2) + 64, :, :]

            def ld(e, g=g, q=q, dst=dst):
                off = base_off(e) + g * 512
                return e.dma_start(out=dst, in_=g2v[q][:, :, bass.ds(off, 512)])
            S.add('sp', ld, r=[('gout2', q)], w=['Yg' if q < 2 else 'At'], inc=16, semname=('d', 'YgAt'))
        S.dma('sp', h1b[:], g1v[g][:, :, :], r=[('gin1', t) for t in range(4 * g, 4 * g + 4)], w=['h1b'], ch='h1b')
        for m in range(4):
            p = px % 2
            px += 1
            for r_ in range(4):
                S.add('pe', lambda e, m=m, r_=r_, p=p: e.matmul(PX[p][:], lhsT=glw[:, r_, m * 128:(m + 1) * 128], rhs=Yg[:, r_, :], start=(r_ == 0), stop=(r_ == 3)), r=['glw', 'Yg'], w=[('PX', p)])
            S.add('act', lambda e, m=m, p=p: e.activation(out=sgz[:], in_=PX[p][:], func=AF.Sigmoid, bias=glb[:, m:m + 1]), r=[('PX', p), 'glb'], w=['sgz'])
            S.add('dve', lambda e, m=m: e.tensor_tensor(out=ygl[:, m, :], in0=Yg[:, m, :], in1=sgz[:], op=ALU.mult), r=['Yg', 'sgz'], w=[('ygl', m)])
        yk = [('ygl', m) for m in range(4)]
        for dc in range(8):
            q = dc % 2
            dsl = slice(dc * 128, (dc + 1) * 128)
            for m in range(4):
                S.add('pe', lambda e, m=m, dsl=dsl: e.matmul(PAa[:], lhsT=wbs[:, m, dsl], rhs=ygl[:, m, :], start=(m == 0), stop=(m == 3)), r=wbs_k + yk, w=['PAa'])
            for r_ in range(4):
                S.add('pe', lambda e, r_=r_, dsl=dsl: e.matmul(PBb[:], lhsT=wba[:, r_, dsl], rhs=At[:, r_, :], start=(r_ == 0), stop=(r_ == 3)), r=wba_k + ['At'], w=['PBb'])
            for k in range(8):
                S.add('pe', lambda e, k=k, dc=dc: e.matmul(PGs[:], lhsT=wgt[:, k, dc * 128:(dc + 1) * 128], rhs=h1b[:, k, :], start=(k == 0), stop=(k == 7)), r=wgt_k + ['h1b'], w=['PGs'])
            for k in range(8):
                S.add('pe', lambda e, k=k, dc=dc: e.matmul(PGa[:], lhsT=wgt[:, k, 1024 + dc * 128:1024 + (dc + 1) * 128], rhs=h1b[:, k, :], start=(k == 0), stop=(k == 7)), r=wgt_k + ['h1b'], w=['PGa'])
            S.add('act', lambda e, q=q: e.activation(out=sgs[q][:], in_=PGs[:], func=AF.Sigmoid), r=['PGs'], w=[('sgs', q)])
            S.add('act', lambda e, q=q: e.activation(out=sga[q][:], in_=PGa[:], func=AF.Sigmoid), r=['PGa'], w=[('sga', q)])
            S.add('dve', lambda e, q=q: e.tensor_tensor(out=m1[q][:], in0=sgs[q][:], in1=PAa[:], op=ALU.mult), r=[('sgs', q), 'PAa'], w=[('m1', q)])
            S.add('dve', lambda e, q=q: e.tensor_tensor(out=m2[q][:], in0=sga[q][:], in1=PBb[:], op=ALU.mult), r=[('sga', q), 'PBb'], w=[('m2', q)])
            S.add('pool', lambda e, q=q, dc=dc: e.tensor_tensor(out=mT[:, dc, :], in0=m1[q][:], in1=m2[q][:], op=ALU.add), r=[('m1', q), ('m2', q)], w=[('mT', dc)])
        mk = [('mT', dc) for dc in range(8)]
        for tt in range(4):
            t = 4 * g + tt
            s = t % 2
            S.dma('act', x1t[s][:], x1s[t * 128:(t + 1) * 128, :], r=[('x1s', t)], w=[('x1t', s)], ch=('x1t', s))
            for half in range(2):
                p = px % 2
                px += 1
                for dc in range(8):
                    S.add('pe', lambda e, dc=dc, tt=tt, half=half, p=p: e.matmul(PX[p][:], lhsT=mT[:, dc, tt * 128:(tt + 1) * 128], rhs=wo[:, dc, half * 512:(half + 1) * 512], start=(dc == 0), stop=(dc == 7)),
                          r=mk + wo_k, w=[('PX', p)])
                S.add('dve', lambda e, s=s, half=half, p=p: e.tensor_tensor(out=x2t[s][:, half * 512:(half + 1) * 512], in0=PX[p][:], in1=x1t[s][:, half * 512:(half + 1) * 512], op=ALU.add),
                      r=[('PX', p), ('x1t', s)], w=[('x2t', s)])
            S.dma('pool', x2s[t * 128:(t + 1) * 128, :], x2t[s][:], r=[('x2t', s)], w=[('x2s', t)], ch=('x2st', s))
            if pend3:
                tp, sp_ = pend3.pop()
                norm_T(S, P, x2t[sp_][:], ('x2t', sp_), sp_, hT[:, :, tp * 128:(tp + 1) * 128], ('f2', 'hT', tp))
            pend3.append((t, s))
    tp, sp_ = pend3.pop()
    norm_T(S, P, x2t[sp_][:], ('x2t', sp_), sp_, hT[:, :, tp * 128:(tp + 1) * 128], ('f2', 'hT', tp))


def build_program(dbg=0, stop=99):
    nc = bass.Bass("TRN2", target_bir_lowering=False)
    A = {}

    def inp(name, shape, dt=F32):
        A[name] = nc.dram_tensor(name, list(shape), dt, kind="ExternalInput").ap()
        return A[name]

    x = inp("x", [TOK, D])
    for f in ("ffn1", "ffn2"):
        inp(f + "_wg", [D, FF]); inp(f + "_wu", [D, FF]); inp(f + "_wd", [FF, D]); inp(f + "_gain", [128, 8])
    inp("ident_bf", [128, 128], BF16); inp("ident_f", [128, 128]); inp("jmat", [128, 128]); inp("tri", [128, 128], BF16)
    inp("mix_gain", [128, 8]); inp("w2", [D, 768]); inp("cos", [64, SEQ]); inp("sins", [64, SEQ]); inp("ind", [32, SEQ], BF16)
    inp("fut", [128, 32, 32]); inp("negb", [128, 32, 32]); inp("sgn", [128, 2])
    inp("s_are", [128, 8]); inp("s_aim", [128, 8]); inp("s_ldt", [128, 8]); inp("s_d", [128, 1])
    inp("s_T1", [128, 8, 16]); inp("s_T2", [128, 8, 16]); inp("s_CA", [128, 8, 16]); inp("s_CB", [128, 8, 16])
    inp("glu_w", [512, 512]); inp("glu_b", [128, 4]); inp("w_bs", [512, D]); inp("w_ba", [512, D]); inp("w_in", [D, 4096]); inp("w_out", [D, D])
    inp("fin_gain", [128, D])
    out = nc.dram_tensor("out", [TOK, D], F32, kind="ExternalOutput").ap()
    x1s = nc.dram_tensor("x1s", [TOK, D], F32).ap()
    x2s = nc.dram_tensor("x2s", [TOK, D], F32).ap()
    gin1 = [nc.dram_tensor(f"gin1_{q}", [D, 512], BF16).ap() for q in range(4)]
    gout1 = [nc.dram_tensor(f"gout1_{q}", [4 * D, 512], BF16).ap() for q in range(4)]
    gin2 = [nc.dram_tensor(f"gin2_{q}", [64, SEQ], BF16).ap() for q in range(4)]
    gout2 = [nc.dram_tensor(f"gout2_{q}", [4 * 64, SEQ], BF16).ap() for q in range(4)]
    if dbg:
        dbg_g2 = nc.dram_tensor("dbg_g2", [256, SEQ], BF16, kind="ExternalOutput").ap()
        dbg_x2 = nc.dram_tensor("dbg_x2", [TOK, D], F32, kind="ExternalOutput").ap()
    RG = [[0, 1, 2, 3], [4, 5, 6, 7]]

    with ExitStack() as top:
        S = Sched(nc, top)
        with ExitStack() as es:
            S.es = es
            ident = S.sb("ident", [128, 128], BF16)
            S.dma('sp', ident[:], A["ident_bf"][:, :], w=['ident'], ch='misc')
            g1 = S.sb("g1", [128, 8], F32)
            S.dma('sp', g1[:], A["ffn1_gain"][:, :], w=[('f1', 'gain')], ch='misc')
            arena = S.sb("arena", [128, NJ * D], BF16)
            hT = arena[:, 0:8 * TOK].rearrange("p (k n) -> p k n", k=8)
            P = {**alloc_norm(S, "n1"), 'ident': ident}
            xin = [S.sb(f"xin{i}", [128, D], F32) for i in range(2)]
            for t in range(NT):
                s = t % 2
                S.dma('sp', xin[s][:], x[t * 128:(t + 1) * 128, :], w=[('xin', s)], ch=('xin', s))
                norm_T(S, P, xin[s][:], ('xin', s), s, hT[:, :, t * 128:(t + 1) * 128], ('f1', 'hT', t))
            h1T = [S.sb(f"h1T{i}", [128, 8, 128], BF16) for i in range(2)]
            g1v = [g.rearrange("(k p) n -> p k n", p=128) for g in gin1]

            def epi1(t, xn, xkey, s):
                S.dma('pool', x1s[t * 128:(t + 1) * 128, :], xn[:], r=[xkey], w=[('x1s', t)], ch=('x1st', s))
                norm_T(S, P, xn[:], xkey, s, h1T[s][:], ('h1T', s))
                S.dma('pool', g1v[t // 4][:, :, (t % 4) * 128:(t % 4 + 1) * 128], h1T[s][:], r=[('h1T', s)], w=[('gin1', t)], ch=('g1st', s))
                if t % 4 == 3:
                    q = t // 4
                    S.add('pool', lambda e, q=q: e.collective_compute("AllGather", ALU.bypass, replica_groups=RG, ins=[gin1[q].rearrange("(a b) n -> a (b n)", a=128)], outs=[gout1[q].rearrange("(a b) n -> a (b n)", a=512)]),
                          r=[('gin1', tt) for tt in range(4 * q, 4 * q + 4)], w=[('hG', q)], inc=1, semname='cc1')

            ffn_core(S, arena, xin, lambda t: x[t * 128:(t + 1) * 128, :], A["ffn1_wg"], A["ffn1_wu"], A["ffn1_wd"], g1, epi1, 'f1')
            if stop == 1:
                S.fence('sp', [('hG', q) for q in range(4)])
            S.emit()
        if stop == 1:
            return nc
        with ExitStack() as es2:
            S.es = es2
            uT = S.sb("uT", [128, SEQ], BF16)
            with ExitStack() as esq:
                S.es = esq
                QA = [S.sb(f"QA{i}", [96, SEQ], BF16) for i in range(2)]
                KA = [S.sb(f"KA{i}", [96, SEQ], BF16) for i in range(2)]
                VA = [S.sb(f"VA{i}", [128, 64, 128], BF16) for i in range(2)]
                ksum = [S.sb(f"ksum{i}", [64, 32], F32) for i in range(2)]
                with ExitStack() as esa:
                    S.es = esa
                    phase2_proj(S, A, gout1, uT, QA, KA, VA, ksum)
                    S.emit()
                with ExitStack() as esg:
                    S.es = esg
                    phase2_gate(S, A, QA, ksum)
                    S.emit()
                g2k = [[('gin2y', tb, 0) for tb in range(SEQ // TC)], [('gin2y', tb, 1) for tb in range(SEQ // TC)],
                       [('gin2a', 0, i) for i in range(16)], [('gin2a', 1, i) for i in range(16)]]

                def ag2(q):
                    if dbg:
                        S.dma('sp', dbg_g2[64 * q:64 * q + 64, :], gin2[q][:, :], r=g2k[q], w=[('dbg_g2', q)], ch='dbg')
                    S.add('pool', lambda e, q=q: e.collective_compute("AllGather", ALU.bypass, replica_groups=RG, ins=[gin2[q].rearrange("r (x n) -> (r x) n", x=2)], outs=[gout2[q].rearrange("r (x n) -> (r x) n", x=2)]),
                          r=g2k[q], w=[('gout2', q)], inc=1, semname='cc2')

                if _DBG_NOINTER:
                    with ExitStack() as esc:
                        S.es = esc
                        Tn = phase2_ssm_prep(S, A)
                        S.es = esc
                        PCb = S.ps("PCb", [128, 512], F32)
                        for _ in phase2_ssm_main(S, A, uT, gin2, Tn, PCb):
                            pass
                        S.emit()
                    with ExitStack() as esc:
                        S.es = esc
                        ag2(0)
                        ag2(1)
                        PCa = S.ps("PCa", [128, 512], F32)
                        for _ in phase2_attn(S, A, QA, KA, VA, gin2, PCa, after_head=lambda hh: ag2(2 + hh)):
                            pass
                        if stop == 5:
                            S.fence('sp', [('gout2', q) for q in range(4)] + [('dbg_g2', q) for q in range(4)])
                        S.emit()
                else:
                    with ExitStack() as esc:
                        S.es = esc
                        Tn = phase2_ssm_prep(S, A)
                        S.es = esc
                        PCb = S.ps("PCb", [128, 512], F32)
                        PCa = S.ps("PCa", [128, 512], F32)
                        ga = phase2_attn(S, A, QA, KA, VA, gin2, PCa, after_head=lambda hh: ag2(2 + hh), nps=3)
                        gs = phase2_ssm_main(S, A, uT, gin2, Tn, PCb, lean=True)
                        n_att = sum(4 * i + 4 for i in range(16)) * 2
                        n_ssm = (SEQ // TC) * 8 + 3
                        credit = 0.0
                        ssm_done = False
                        for _ in ga:
                            credit += n_ssm / n_att * 1.3
                            while credit >= 1.0 and not ssm_done:
                                credit -= 1.0
                                try:
                                    next(gs)
                                except StopIteration:
                                    ssm_done = True
                                    ag2(0)
                                    ag2(1)
                        if not ssm_done:
                            for _ in gs:
                                pass
                            ag2(0)
                            ag2(1)
                        if stop == 5:
                            S.fence('sp', [('gout2', q) for q in range(4)] + [('dbg_g2', q) for q in range(4)])
                        S.emit()
            if stop == 5:
                return nc
        with ExitStack() as es3:
            S.es = es3
            ident = S.sb("ident3", [128, 128], BF16)
            S.dma('sp', ident[:], A["ident_bf"][:, :], w=['ident'], ch='misc')
            arena = S.sb("arena3", [128, NJ * D], BF16)
            with ExitStack() as es3a:
                S.es = es3a
                P = {**alloc_norm(S, "n3"), 'ident': ident}
                phase3_mix(S, A, gin1, gout2, x1s, x2s, arena, P)
                if dbg:
                    S.dma('sp', dbg_x2[:, :], x2s[:, :], r=[('x2s', t) for t in range(NT)], w=['dbg_x2'], ch='dbg')
                S.emit()
            with ExitStack() as es3b:
                S.es = es3b
                g2 = S.sb("g2", [128, 8], F32)
                S.dma('sp', g2[:], A["ffn2_gain"][:, :], w=[('f2', 'gain')], ch='misc')
                fg = S.sb("fg", [128, D], F32)
                S.dma('sp', fg[:], A["fin_gain"][:, :], w=['fg'], ch='misc')
                xr = [S.sb(f"xr3_{i}", [128, D], F32) for i in range(2)]
                junk = [S.sb(f"fjunk{i}", [128, D], BF16) for i in range(2)]
                ssq = [S.sb(f"fssq{i}", [128, 1], F32) for i in range(2)]
                ot = [S.sb(f"fot{i}", [128, D], F32) for i in range(2)]

                def epi3(t, xn, xkey, s):
                    kq = ('fssq', s)
                    S.add('act', lambda e: e.activation(out=junk[s][:], in_=xn[:], func=AF.Square, accum_out=ssq[s][:]), r=[xkey], w=[('fjunk', s), kq])
                    S.add('dve', lambda e: e.tensor_scalar(out=ssq[s][:], in0=ssq[s][:], scalar1=1.0 / D, scalar2=EPS, op0=ALU.mult, op1=ALU.add), r=[kq], w=[kq])
                    S.add('act', lambda e: e.activation(out=ssq[s][:], in_=ssq[s][:], func=AF.Sqrt), r=[kq], w=[kq])
                    S.add('dve', lambda e: e.reciprocal(out=ssq[s][:], in_=ssq[s][:]), r=[kq], w=[kq])
                    S.add('dve', lambda e: e.scalar_tensor_tensor(out=ot[s][:], in0=xn[:], scalar=ssq[s][:, 0:1], in1=fg[:], op0=ALU.mult, op1=ALU.mult), r=[xkey, kq, 'fg'], w=[('fot', s)])
                    S.dma('pool', out[t * 128:(t + 1) * 128, :], ot[s][:], r=[('fot', s)], w=[('out', t)], ch=('ost', s))

                ffn_core(S, arena, xr, lambda t: x2s[t * 128:(t + 1) * 128, :], A["ffn2_wg"], A["ffn2_wu"], A["ffn2_wd"], g2, epi3, 'f2')
                fk = [('out', t) for t in range(NT)]
                if dbg:
                    fk += [('dbg_g2', q) for q in range(4)] + ['dbg_x2']
                S.fence('sp', fk)
                S.emit()
    return nc


def _perm_sw():
    return np.concatenate([np.arange(32, 64), np.arange(0, 32)])


def make_in_maps(inputs):
    f32 = np.float32
    bf = ml_dtypes.bfloat16
    x = np.asarray(inputs["x"], f32).reshape(8, TOK, D)
    w_in = np.asarray(inputs["w_in"], f32)[0]

    def gl(g):
        return np.ascontiguousarray(np.asarray(g, f32).reshape(8, 128).T)

    ident = np.eye(128, dtype=f32)
    jmat = np.zeros((128, 128), f32)
    jmat[np.arange(128), (np.arange(128) + 64) % 128] = 1.0
    tri = (np.arange(128)[:, None] <= np.arange(128)[None, :]).astype(f32)
    pos = np.arange(SEQ, dtype=f32)
    inv = (10000.0 ** (-np.arange(0, 64, 2, dtype=f32) / 64.0)).astype(f32)
    ang = (pos[None, :] * inv[:, None]).astype(f32)
    cos = np.cos(ang).astype(f32)
    sin = np.sin(ang).astype(f32)
    COS = np.ascontiguousarray(np.concatenate([cos, cos], 0))
    SINS = np.ascontiguousarray(np.concatenate([-sin, sin], 0))
    ind = (np.arange(SEQ)[None, :] // 256 == np.arange(32)[:, None]).astype(f32).astype(bf)
    fut = np.zeros((128, 32, 32), f32)
    negb = np.full((128, 32, 32), NEG_MASK, f32)
    for cur in range(32):
        fut[:, cur, cur:] = -1e30
        negb[:, cur, cur] = 0.0
    sgn = np.ones((128, 2), f32)
    sgn[64:, 0] = -1.0
    sgn[:64, 1] = -1.0
    common = {
        "ident_bf": ident.astype(bf), "ident_f": ident, "jmat": jmat, "tri": tri.astype(bf), "cos": COS, "sins": SINS, "ind": ind,
        "fut": fut, "negb": negb, "sgn": sgn,
        "mix_gain": gl(inputs["mix_norm"][0]),
        "glu_w": np.asarray(inputs["glu_w"], f32)[0], "glu_b": np.ascontiguousarray(np.asarray(inputs["glu_b"], f32)[0].reshape(4, 128).T),
        "w_bs": np.asarray(inputs["w_branch_ssm"], f32)[0], "w_ba": np.asarray(inputs["w_branch_attn"], f32)[0],
        "w_in": w_in, "w_out": np.asarray(inputs["w_out"], f32)[0],
        "fin_gain": np.ascontiguousarray(np.broadcast_to(np.asarray(inputs["final_norm"], f32)[None, :], (128, D))),
    }
    for f in ("ffn1", "ffn2"):
        common[f + "_wg"] = np.asarray(inputs[f + "_w_gate"], f32)[0]
        common[f + "_wu"] = np.asarray(inputs[f + "_w_up"], f32)[0]
        common[f + "_wd"] = np.asarray(inputs[f + "_w_down"], f32)[0]
        common[f + "_gain"] = gl(inputs[f + "_norm"][0])
    a_re = np.asarray(inputs["ssm_a_re"], f32)[0]; a_im = np.asarray(inputs["ssm_a_im"], f32)[0]
    b_re = np.asarray(inputs["ssm_b_re"], f32)[0]; b_im = np.asarray(inputs["ssm_b_im"], f32)[0]
    c_re = np.asarray(inputs["ssm_c_re"], f32)[0]; c_im = np.asarray(inputs["ssm_c_im"], f32)[0]
    ldt = np.asarray(inputs["ssm_log_dt"], f32)[0]; dd = np.asarray(inputs["ssm_d"], f32)[0]
    sw = _perm_sw()
    maps = []
    for c in range(8):
        cs = c % 4
        m = dict(common)
        m["x"] = x[c]
        ucols = np.arange(cs * 128, cs * 128 + 128)
        qcols = 512 + ucols
        kcols = 1024 + ucols
        vcols = 1536 + ucols
        qsw = np.concatenate([qcols[0:64][sw], qcols[64:128][sw]])
        ksw = np.concatenate([kcols[0:64][sw], kcols[64:128][sw]])
        m["w2"] = np.ascontiguousarray(w_in[:, np.concatenate([ucols, qcols, qsw, kcols, ksw, vcols])])
        gs = slice(cs * 8, cs * 8 + 8)
        m["s_are"] = np.ascontiguousarray(np.concatenate([a_re[gs].T, a_re[gs].T], 0))
        m["s_aim"] = np.ascontiguousarray(np.concatenate([a_im[gs].T, a_im[gs].T], 0))
        m["s_ldt"] = np.ascontiguousarray(np.broadcast_to(ldt[gs][None, :], (128, 8)))
        m["s_d"] = np.ascontiguousarray(dd[cs * 128:(cs + 1) * 128].reshape(128, 1))
        bre = b_re[gs].transpose(1, 0, 2); bim = b_im[gs].transpose(1, 0, 2)
        m["s_T1"] = np.ascontiguousarray(np.concatenate([bre, bim], 0))
        m["s_T2"] = np.ascontiguousarray(np.concatenate([bim, bre], 0))
        cre = c_re[gs].transpose(2, 0, 1); cim = c_im[gs].transpose(2, 0, 1)
        m["s_CA"] = np.ascontiguousarray(np.concatenate([cre, cim], 0))
        m["s_CB"] = np.ascontiguousarray(np.concatenate([cim, cre], 0))
        maps.append(m)
    return maps


_NC_CACHE = {}


def kernel(**inputs):
    if 0 not in _NC_CACHE:
        _NC_CACHE[0] = build_program(0)
    nc = _NC_CACHE[0]
    maps = make_in_maps(inputs)
    res = run_bass_kernel_spmd(nc, maps, core_ids=list(range(8)))
    outs = [np.asarray(res.results[c]["out"], dtype=np.float32) for c in range(8)]
    return np.stack(outs, 0).reshape(2, SEQ, D)
```

```python
import numpy as np
import ml_dtypes
from contextlib import ExitStack
import concourse.bass as bass
import concourse.mybir as mybir
from concourse.bass_utils import run_bass_kernel_spmd

F32 = mybir.dt.float32
BF16 = mybir.dt.bfloat16
AF = mybir.ActivationFunctionType
ALU = mybir.AluOpType
AX = mybir.AxisListType

SEM_LIMIT = 30000
import os as _os
_DBG_NOCONS = _os.environ.get("KOPT_NOCONS", "0") == "1"
_DBG_NOINTER = _os.environ.get("KOPT_NOINTER", "1") == "1"


class Sched:
    def __init__(self, nc, es):
        self.nc = nc
        self.es = es
        self.sem_es = es
        self.eng = {'pe': nc.tensor, 'act': nc.scalar, 'dve': nc.vector, 'pool': nc.gpsimd, 'sp': nc.sync}
        self.ops = {e: [] for e in self.eng}
        self.cnt = {}
        self.epoch = {}
        self.res = {}
        self.seen = {e: {} for e in self.eng}
        self.semh = {}
        self.nsb = 0
        self.fenced = {}

    def sb(self, name, shape, dtype):
        return self.es.enter_context(self.nc.sbuf_tensor("sb_" + name, list(shape), dtype))

    def ps(self, name, shape, dtype):
        return self.es.enter_context(self.nc.psum_tensor("ps_" + name, list(shape), dtype))

    def _semkey(self, base, inc):
        ep = self.epoch.get(base, 0)
        key = (base, ep)
        if self.cnt.get(key, 0) + inc > SEM_LIMIT:
            ep += 1
            self.epoch[base] = ep
            key = (base, ep)
        return key

    def add(self, eng, fn, r=(), w=(), inc=1, semname=None):
        waits = {}

        def need(tok):
            for k, v in tok.items():
                if waits.get(k, 0) < v:
                    waits[k] = v

        for key in r:
            st = self.res.get(key)
            if st is not None:
                need(st[0])
        for key in w:
            st = self.res.get(key)
            if st is not None:
                need(st[0])
                need(st[1])
        base = semname if semname is not None else ('c', eng)
        semkey = self._semkey(base, inc)
        cnt = self.cnt.get(semkey, 0) + inc
        self.cnt[semkey] = cnt
        final = []
        for k, v in waits.items():
            if isinstance(k[0], tuple) and k[0][0] == 'd' and not _DBG_NOCONS:
                v = max(v, self.cnt.get(k, 0) - (inc if k == semkey else 0))
            if self.seen[eng].get(k, 0) >= v:
                continue
            if eng == 'pe' and k[0] == ('c', 'pe'):
                continue
            self.seen[eng][k] = v
            final.append((k, v))
        self.ops[eng].append((fn, final, semkey, inc))
        tok = {semkey: cnt}
        for key in r:
            st = self.res.setdefault(key, [{}, {}])
            if st[1].get(semkey, 0) < cnt:
                st[1][semkey] = cnt
        for key in w:
            self.res[key] = [dict(tok), {}]
        return tok

    def dma(self, eng, out, in_, r=(), w=(), ch=None, **kw):
        return self.add(eng, lambda e: e.dma_start(out=out, in_=in_, **kw), r=r, w=w, inc=16,
                        semname=('d', ch if ch is not None else eng))

    def fence(self, eng, keys):
        waits = {}
        for key in keys:
            st = self.res.get(key)
            if st is not None:
                for d in st:
                    for k, v in d.items():
                        if waits.get(k, 0) < v:
                            waits[k] = v
        self.ops[eng].append((None, list(waits.items()), None, 0))

    def _sem(self, key):
        h = self.semh.get(key)
        if h is None:
            h = self.sem_es.enter_context(self.nc.semaphore(f"s{len(self.semh)}"))
            self.semh[key] = h
        return h

    def emit(self):
        fw = []
        for k, v in self.cnt.items():
            if isinstance(k[0], tuple) and k[0][0] == 'd' and self.fenced.get(k, 0) < v:
                self.fenced[k] = v
                if self.seen['sp'].get(k, 0) < v:
                    self.seen['sp'][k] = v
                    fw.append((k, v))
        if fw:
            self.ops['sp'].append((None, fw, None, 0))
        for e, ops in self.ops.items():
            for fn, waits, semkey, inc in ops:
                for k, v in waits:
                    self._sem(k)
                if semkey is not None:
                    self._sem(semkey)
        with self.nc.Block() as block:
            self._emit_block(block)
        self.ops = {e: [] for e in self.eng}

    def _emit_block(self, block):
        deco = {'pe': block.tensor, 'act': block.scalar, 'dve': block.vector, 'pool': block.gpsimd, 'sp': block.sync}

        def mk(ename):
            ops = self.ops[ename]

            def body(e):
                for fn, waits, semkey, inc in ops:
                    for k, v in waits:
                        e.wait_ge(self.semh[k], v)
                    if fn is not None:
                        fn(e).then_inc(self.semh[semkey], inc)
            return body

        for ename in self.eng:
            if self.ops[ename]:
                deco[ename](mk(ename))


D = 1024
FF = 2816
NJ = FF // 128
TOK = 2048
NT = TOK // 128
SEQ = 8192
EPS = 1e-6


def norm_T(S, P, xt, xkey, par, dst, dkey):
    junk, ssq, rstd, hb, ptr = P['junk'][par], P['ssq'][par], P['rstd'][par], P['hb'][par], P['ptr'][par]
    kj, ks, kr, kh, kp = ('junk', par), ('ssq', par), ('rstd', par), ('hb', par), ('ptr', par)
    S.add('act', lambda e: e.activation(out=junk[:], in_=xt, func=AF.Square, accum_out=ssq[:]), r=[xkey], w=[kj, ks])
    S.add('dve', lambda e: e.tensor_scalar(out=rstd[:], in0=ssq[:], scalar1=1.0 / D, scalar2=EPS, op0=ALU.mult, op1=ALU.add), r=[ks], w=[kr])
    S.add('act', lambda e: e.activation(out=rstd[:], in_=rstd[:], func=AF.Sqrt), r=[kr], w=[kr])
    S.add('dve', lambda e: e.reciprocal(out=rstd[:], in_=rstd[:]), r=[kr], w=[kr])
    S.add('dve', lambda e: e.tensor_scalar(out=hb[:], in0=xt, scalar1=rstd[:, 0:1], scalar2=None, op0=ALU.mult), r=[xkey, kr], w=[kh])
    for k in range(8):
        S.add('pe', lambda e, k=k: e.transpose(out=ptr[:, k, :], in_=hb[:, k * 128:(k + 1) * 128], identity=P['ident'][:]), r=[kh, 'ident'], w=[kp])
    S.add('dve', lambda e: e.tensor_copy(out=dst, in_=ptr[:]), r=[kp], w=[dkey])
    return rstd, kr


def alloc_norm(S, tag):
    P = {'junk': [], 'ssq': [], 'rstd': [], 'hb': [], 'ptr': []}
    for par in range(2):
        P['junk'].append(S.sb(f"{tag}junk{par}", [128, D], BF16))
        P['ssq'].append(S.sb(f"{tag}ssq{par}", [128, 1], F32))
        P['rstd'].append(S.sb(f"{tag}rstd{par}", [128, 1], F32))
        P['hb'].append(S.sb(f"{tag}hb{par}", [128, D], BF16))
        P['ptr'].append(S.ps(f"{tag}ptr{par}", [128, 8, 128], BF16))
    return P


def ffn_core(S, arena, xr, x_src, wg, wu, wd, gain_t, epilogue, tag):
    aT = S.sb(f"{tag}aT", [128, NJ, TOK], BF16)
    hT = arena[:, 0:8 * TOK].rearrange("p (k n) -> p k n", k=8)
    wd_bf = arena[:, :].rearrange("p (j d) -> p j d", j=NJ)
    hT_keys = [(tag, 'hT', t) for t in range(NT)]
    wg_st = [S.sb(f"{tag}wgst{i}", [128, 8, 128], F32) for i in range(2)]
    wu_st = [S.sb(f"{tag}wust{i}", [128, 8, 128], F32) for i in range(2)]
    wg_bf = [S.sb(f"{tag}wgbf{i}", [128, 8, 128], BF16) for i in range(2)]
    wu_bf = [S.sb(f"{tag}wubf{i}", [128, 8, 128], BF16) for i in range(2)]
    wd_st = [S.sb(f"{tag}wdst{i}", [128, D], F32) for i in range(2)]
    sg = [S.sb(f"{tag}sg{i}", [128, 512], F32) for i in range(2)]
    xn = [S.sb(f"{tag}xn{i}", [128, D], F32) for i in range(2)]
    G = [S.ps(f"{tag}G{i}", [128, 512], F32) for i in range(2)]
    U = [S.ps(f"{tag}U{i}", [128, 512], F32) for i in range(2)]
    Y = [S.ps(f"{tag}Y{i}", [128, 512], F32) for i in range(2)]
    wgv = wg.rearrange("(k p) f -> p k f", p=128)
    wuv = wu.rearrange("(k p) f -> p k f", p=128)
    gbc = gain_t[:].unsqueeze(2).broadcast_to([128, 8, 128])
    it = 0
    for j in range(NJ):
        s = j % 2
        S.dma('sp', wg_st[s][:], wgv[:, :, j * 128:(j + 1) * 128], w=[(tag, 'wgst', s)], ch=(tag, 'wg', s))
        S.dma('sp', wu_st[s][:], wuv[:, :, j * 128:(j + 1) * 128], w=[(tag, 'wust', s)], ch=(tag, 'wu', s))
        S.add('dve', lambda e, s=s: e.tensor_tensor(out=wg_bf[s][:], in0=wg_st[s][:], in1=gbc, op=ALU.mult),
              r=[(tag, 'wgst', s), (tag, 'gain')], w=[(tag, 'wgbf', s)])
        S.add('pool', lambda e, s=s: e.tensor_tensor(out=wu_bf[s][:], in0=wu_st[s][:], in1=gbc, op=ALU.mult),
              r=[(tag, 'wust', s), (tag, 'gain')], w=[(tag, 'wubf', s)])
        for g4 in range(4):
            pb = it % 2
            it += 1
            hk = [(tag, 'hT', t) for t in range(4 * g4, 4 * g4 + 4)]
            for k in range(8):
                S.add('pe', lambda e, s=s, k=k, g4=g4, pb=pb: e.matmul(G[pb][:], lhsT=wg_bf[s][:, k, :], rhs=hT[:, k, g4 * 512:(g4 + 1) * 512], start=(k == 0), stop=(k == 7)),
                      r=[(tag, 'wgbf', s)] + hk, w=[(tag, 'G', pb)])
            for k in range(8):
                S.add('pe', lambda e, s=s, k=k, g4=g4, pb=pb: e.matmul(U[pb][:], lhsT=wu_bf[s][:, k, :], rhs=hT[:, k, g4 * 512:(g4 + 1) * 512], start=(k == 0), stop=(k == 7)),
                      r=[(tag, 'wubf', s)] + hk, w=[(tag, 'U', pb)])
            S.add('act', lambda e, pb=pb: e.activation(out=sg[pb][:], in_=G[pb][:], func=AF.Silu), r=[(tag, 'G', pb)], w=[(tag, 'sg', pb)])
            S.add('dve', lambda e, pb=pb, j=j, g4=g4: e.tensor_tensor(out=aT[:, j, g4 * 512:(g4 + 1) * 512], in0=sg[pb][:], in1=U[pb][:], op=ALU.mult),
                  r=[(tag, 'sg', pb), (tag, 'U', pb)], w=[(tag, 'aT', j, g4)])
    for j in range(NJ):
        s = j % 2
        S.dma('sp', wd_st[s][:], wd[j * 128:(j + 1) * 128, :], w=[(tag, 'wdst', s)], ch=(tag, 'wd', s))
        if j % 2 == 0:
            S.add('act', lambda e, s=s, j=j: e.activation(out=wd_bf[:, j, :], in_=wd_st[s][:], func=AF.Copy),
                  r=[(tag, 'wdst', s)], w=[(tag, 'wdbf', j)] + hT_keys)
        else:
            S.add('dve', lambda e, s=s, j=j: e.tensor_copy(out=wd_bf[:, j, :], in_=wd_st[s][:]),
                  r=[(tag, 'wdst', s)], w=[(tag, 'wdbf', j)] + hT_keys)
    it = 0
    pending = None
    for t in range(NT):
        s = t % 2
        S.dma('sp', xr[s][:], x_src(t), r=[(tag, 'xsrc', t)], w=[(tag, 'xr', s)], ch=(tag, 'xr', s))
        for half in range(2):
            pb = it % 2
            it += 1
            for j in range(NJ):
                S.add('pe', lambda e, j=j, t=t, half=half, pb=pb: e.matmul(Y[pb][:], lhsT=aT[:, j, t * 128:(t + 1) * 128], rhs=wd_bf[:, j, half * 512:(half + 1) * 512], start=(j == 0), stop=(j == NJ - 1)),
                      r=[(tag, 'aT', j, t // 4), (tag, 'wdbf', j)], w=[(tag, 'Y', pb)])
            S.add('dve', lambda e, s=s, half=half, pb=pb: e.scalar_tensor_tensor(out=xn[s][:, half * 512:(half + 1) * 512], in0=Y[pb][:], scalar=0.5, in1=xr[s][:, half * 512:(half + 1) * 512], op0=ALU.mult, op1=ALU.add),
                  r=[(tag, 'Y', pb), (tag, 'xr', s)], w=[(tag, 'xn', s)])
        if pending is not None:
            epilogue(*pending)
        pending = (t, xn[s], (tag, 'xn', s), s)
    epilogue(*pending)


NEG_MASK = -960.0


def phase2_proj(S, A, hG, uT, QA, KA, VA, ksum):
    w2st = [S.sb(f"w2st{i}", [128, 768], F32) for i in range(2)]
    w2b = S.sb("w2b", [128, 8, 768], BF16)
    gm = S.sb("gmix", [128, 8], F32)
    hblk = [S.sb(f"hblk{i}", [128, 8, 512], BF16) for i in range(2)]
    cosb = [S.sb(f"cosb{i}", [64, 512], F32) for i in range(2)]
    sinb = [S.sb(f"sinb{i}", [64, 512], F32) for i in range(2)]
    t1 = [S.sb(f"rt1_{i}", [64, 512], F32) for i in range(2)]
    t2 = [S.sb(f"rt2_{i}", [64, 512], F32) for i in range(2)]
    rot = [S.sb(f"rrot{i}", [64, 512], F32) for i in range(2)]
    PU = S.ps("PU", [128, 512], F32)
    PA = [S.ps(f"PA{i}", [128, 512], F32) for i in range(2)]
    PB = [S.ps(f"PB{i}", [128, 512], F32) for i in range(2)]
    PV = [S.ps(f"PV{i}", [128, 512], F32) for i in range(2)]
    S.dma('sp', gm[:], A["mix_gain"][:, :], w=['gmix'], ch='misc')
    w2v = A["w2"].rearrange("(k p) f -> p k f", p=128)
    for k in range(8):
        s = k % 2
        S.dma('sp', w2st[s][:], w2v[:, k, :], w=[('w2st', s)], ch=('w2', s))
        S.add('dve', lambda e, s=s, k=k: e.tensor_scalar(out=w2b[:, k, :], in0=w2st[s][:], scalar1=gm[:, k:k + 1], scalar2=None, op0=ALU.mult),
              r=[('w2st', s), 'gmix'], w=[('w2b', k)])
    for hh in range(2):
        S.add('pool', lambda e, hh=hh: e.memset(VA[hh][:, :, 64 * (1 - hh):64 * (1 - hh) + 64], 1.0), w=[('VA1', hh)])
        S.dma('act', KA[hh][64:96, :], A["ind"][:, :], w=[('KAind', hh)], ch='misc2')
        S.add('pool', lambda e, hh=hh: e.memset(ksum[hh][:], 0.0), w=[('ksum', hh)])
    w2keys = [('w2b', k) for k in range(8)]
    hgv = [g.rearrange("(r k p) n -> r p k n", r=4, k=8, p=128) for g in hG]
    pp = 0
    pv = 0
    for tb in range(16):
        r, c0 = tb // 4, (tb % 4) * 512
        s = tb % 2
        sl = slice(tb * 512, (tb + 1) * 512)
        S.dma('sp', hblk[s][:], hgv[tb % 4][r][:, :, :], r=[('hG', tb % 4)], w=[('hblk', s)], ch=('hblk', s))
        S.dma('act', cosb[s][:], A["cos"][:, sl], w=[('cosb', s)], ch=('cos', s))
        S.dma('act', sinb[s][:], A["sins"][:, sl], w=[('sinb', s)], ch=('sin', s))
        for k in range(8):
            S.add('pe', lambda e, s=s, k=k: e.matmul(PU[:], lhsT=w2b[:, k, 0:128], rhs=hblk[s][:, k, :], start=(k == 0), stop=(k == 7)),
                  r=w2keys + [('hblk', s)], w=['PU'])
        S.add('act', lambda e, sl=sl: e.activation(out=uT[:, sl], in_=PU[:], func=AF.Copy), r=['PU'], w=[('uT', tb)])
        for (nm, cm, cs_, dst) in (('k', 384, 512, KA), ('q', 128, 256, QA)):
            for hh in range(2):
                p = pp % 2
                pp += 1
                for k in range(8):
                    S.add('pe', lambda e, s=s, k=k, p=p, c=cm + 64 * hh: e.matmul(PA[p][0:64, :], lhsT=w2b[:, k, c:c + 64], rhs=hblk[s][:, k, :], start=(k == 0), stop=(k == 7)),
                          r=w2keys + [('hblk', s)], w=[('PA', p)])
                for k in range(8):
                    S.add('pe', lambda e, s=s, k=k, p=p, c=cs_ + 64 * hh: e.matmul(PB[p][0:64, :], lhsT=w2b[:, k, c:c + 64], rhs=hblk[s][:, k, :], start=(k == 0), stop=(k == 7)),
                          r=w2keys + [('hblk', s)], w=[('PB', p)])
                S.add('dve', lambda e, s=s, p=p: e.tensor_tensor(out=t1[p][:], in0=PA[p][0:64, :], in1=cosb[s][:], op=ALU.mult), r=[('PA', p), ('cosb', s)], w=[('rt1', p)])
                S.add('dve', lambda e, s=s, p=p: e.tensor_tensor(out=t2[p][:], in0=PB[p][0:64, :], in1=sinb[s][:], op=ALU.mult), r=[('PB', p), ('sinb', s)], w=[('rt2', p)])
                S.add('pool', lambda e, p=p: e.tensor_tensor(out=rot[p][:], in0=t1[p][:], in1=t2[p][:], op=ALU.add), r=[('rt1', p), ('rt2', p)], w=[('rrot', p)])
                S.add('act', lambda e, p=p, hh=hh, dst=dst, sl=sl: e.activation(out=dst[hh][0:64, sl], in_=rot[p][:], func=AF.Copy), r=[('rrot', p)], w=[(nm + 'A', hh, tb)])
                if nm == 'k':
                    S.add('dve', lambda e, p=p, hh=hh, tb=tb: e.tensor_reduce(out=ksum[hh][:, 2 * tb:2 * tb + 2], in_=rot[p][:].rearrange("p (a b) -> p a b", a=2), axis=AX.X, op=ALU.add),
                          r=[('rrot', p)], w=[('ksum', hh)])
        for tt in range(4):
            kt = tb * 4 + tt
            v = pv % 2
            pv += 1
            for k in range(8):
                S.add('pe', lambda e, s=s, k=k, v=v, tt=tt: e.matmul(PV[v][:, 0:128], lhsT=hblk[s][:, k, tt * 128:(tt + 1) * 128], rhs=w2b[:, k, 640:768], start=(k == 0), stop=(k == 7)),
                      r=w2keys + [('hblk', s)], w=[('PV', v)])
            S.add('act', lambda e, v=v, kt=kt: e.activation(out=VA[0][:, kt, 0:64], in_=PV[v][:, 0:64], func=AF.Copy), r=[('PV', v)], w=[('VA', 0, kt)])
            S.add('act', lambda e, v=v, kt=kt: e.activation(out=VA[1][:, kt, 64:128], in_=PV[v][:, 64:128], func=AF.Copy), r=[('PV', v)], w=[('VA', 1, kt)])


def phase2_gate(S, A, QA, ksum):
    fut = S.sb("fut", [128, 32, 32], F32)
    negb = S.sb("negb", [128, 32, 32], F32)
    identf = S.sb("identf", [128, 128], F32)
    kmb = [S.sb(f"kmb{i}", [64, 32], BF16) for i in range(2)]
    gmt = [S.sb(f"gmt{i}", [128, 32], F32) for i in range(4)]
    m8 = [S.sb(f"m8_{i}", [128, 8], F32) for i in range(4)]
    mbw = [S.sb(f"mbw{i}", [128, 96], F32) for i in range(4)]
    PG = [S.ps(f"PG{i}", [128, 512], F32) for i in range(4)]
    PT = [S.ps(f"PT{i}", [128, 512], F32) for i in range(4)]
    S.dma('sp', fut[:], A["fut"][:, :, :], w=['fut'], ch='misc')
    S.dma('sp', negb[:], A["negb"][:, :, :], w=['negb'], ch='misc')
    S.dma('sp', identf[:], A["ident_f"][:, :], w=['identf'], ch='misc')
    for i in range(4):
        S.add('pool', lambda e, i=i: e.memset(mbw[i][:], 0.0), w=[('mbw', i)])
    for i in range(2):
        S.add('dve', lambda e, i=i: e.tensor_copy(out=kmb[i][:], in_=ksum[i][:]), r=[('ksum', i)], w=[('kmb', i)])
    NBUF = 4
    items = [(hh, c) for hh in range(2) for c in range(64)]

    def stA(k):
        hh, c = items[k]
        p = k % NBUF
        csl = slice(c * 128, (c + 1) * 128)
        S.add('pe', lambda e: e.matmul(PG[p][:, 0:32], lhsT=QA[hh][0:64, csl], rhs=kmb[hh][:, :], start=True, stop=True),
              r=[('qA', hh, c // 4), ('kmb', hh)], w=[('PG', p)])

    def stB(k):
        hh, c = items[k]
        cur = c // 2
        p = k % NBUF
        S.add('dve', lambda e: e.tensor_tensor(out=gmt[p][:], in0=PG[p][:, 0:32], in1=fut[:, cur, :], op=ALU.add), r=[('PG', p), 'fut'], w=[('gmt', p)])
        S.add('dve', lambda e: e.max(out=m8[p][:], in_=gmt[p][:]), r=[('gmt', p)], w=[('m8', p)])

    def stC(k):
        hh, c = items[k]
        cur = c // 2
        p = k % NBUF
        S.add('dve', lambda e: e.scalar_tensor_tensor(out=mbw[p][:, 64:96], in0=gmt[p][:], scalar=m8[p][:, 2:3], in1=negb[:, cur, :], op0=ALU.is_lt, op1=ALU.mult),
              r=[('gmt', p), ('m8', p), 'negb'], w=[('mbw', p)])

    def stD(k):
        p = k % NBUF
        S.add('pe', lambda e: e.transpose(out=PT[p][0:96, 0:128], in_=mbw[p][:, :], identity=identf[:]), r=[('mbw', p), 'identf'], w=[('PT', p)])

    def stE(k):
        hh, c = items[k]
        p = k % NBUF
        csl = slice(c * 128, (c + 1) * 128)
        S.add('act', lambda e: e.activation(out=QA[hh][64:96, csl], in_=PT[p][64:96, 0:128], func=AF.Copy), r=[('PT', p)], w=[('qM', hh, c)])

    n = len(items)
    for step in range(n + 3):
        for lag, st in ((0, stA), (1, stB), (2, stC), (2, stD), (3, stE)):
            k = step - lag
            if 0 <= k < n:
                st(k)


def phase2_attn(S, A, QA, KA, VA, gin2, PBc1, after_head=None, nps=4):
    tri = S.sb("tri", [128, 128], BF16)
    ones = S.sb("ones1", [128, 128], F32)
    Pt = [S.sb(f"Pt{i}", [128, 512], BF16) for i in range(nps)]
    Osb = [S.sb(f"Osb{i}", [128, 512], F32) for i in range(2)]
    Ao = [S.sb(f"Ao{i}", [128, 512], BF16) for i in range(2)]
    PS = [S.ps(f"PS{i}", [128, 512], F32) for i in range(nps)]
    PO = [S.ps(f"PO{i}", [128, 512], F32) for i in range(2)]
    S.dma('sp', tri[:], A["tri"][:, :], w=['tri'], ch='misc')
    S.add('pool', lambda e: e.memset(ones[:], 1.0), w=['ones1'])
    it = 0
    gi = 0
    for hh in range(2):
        lo, hi = (0, 64) if hh == 0 else (64, 128)
        dn = 64 if hh == 0 else 0
        for i in range(16):
            g = gi % 2
            gi += 1
            nkt = 4 * i + 4
            qkeys = [('qA', hh, i)] + [('qM', hh, c) for c in range(4 * i, 4 * i + 4)]
            def smm(kt, p):
                nq0 = max(0, kt - 4 * i)
                c0 = nq0 * 128
                qsl = slice(i * 512 + c0, (i + 1) * 512)
                ksl = slice(kt * 128, (kt + 1) * 128)
                S.add('pe', lambda e, hh=hh, p=p, c0=c0, qsl=qsl, ksl=ksl: e.matmul(PS[p][:, c0:512], lhsT=KA[hh][0:96, ksl], rhs=QA[hh][0:96, qsl], start=True, stop=True),
                      r=qkeys + [('kA', hh, kt // 4), ('KAind', hh)], w=[('PS', p)])

            ahead = nps - 2
            for a_ in range(min(ahead, nkt)):
                smm(a_, (it + a_) % nps)
            for kt in range(nkt):
                p = it % nps
                it += 1
                nq0 = max(0, kt - 4 * i)
                c0 = nq0 * 128
                if kt + ahead < nkt:
                    smm(kt + ahead, (it + ahead - 1) % nps)
                S.add('act', lambda e, p=p, c0=c0: e.activation(out=Pt[p][:, c0:512], in_=PS[p][:, c0:512], func=AF.Exp, scale=0.125), r=[('PS', p)], w=[('Pt', p)])
                if kt >= 4 * i:
                    S.add('pool', lambda e, p=p, c0=c0: e.tensor_tensor(out=Pt[p][:, c0:c0 + 128], in0=Pt[p][:, c0:c0 + 128], in1=tri[:], op=ALU.mult), r=[('Pt', p), 'tri'], w=[('Pt', p)])
                S.add('pe', lambda e, hh=hh, p=p, g=g, c0=c0, kt=kt, nkt=nkt: e.matmul(PO[g][:, c0:512], lhsT=VA[hh][:, kt, :], rhs=Pt[p][:, c0:512], start=(kt == 0), stop=(kt == nkt - 1)),
                      r=[('Pt', p), ('VA', hh, kt), ('VA1', hh)], w=[('PO', g)])
                yield
            S.add('act', lambda e, g=g: e.activation(out=Osb[g][:], in_=PO[g][:], func=AF.Copy), r=[('PO', g)], w=[('Osb', g)])
            S.add('dve', lambda e, g=g, dn=dn: e.reciprocal(out=Osb[g][dn:dn + 1, :], in_=Osb[g][dn:dn + 1, :]), r=[('Osb', g)], w=[('Osb', g)])
            S.add('pe', lambda e, g=g, dn=dn, hi=hi: e.matmul(PBc1[0:hi, :], lhsT=ones[dn:dn + 1, 0:hi], rhs=Osb[g][dn:dn + 1, :], start=True, stop=True),
                  r=[('Osb', g), 'ones1'], w=['PCbank'])
            S.add('dve', lambda e, g=g, lo=lo, hi=hi: e.tensor_tensor(out=Ao[g][lo:hi, :], in0=Osb[g][lo:hi, :], in1=PBc1[lo:hi, :], op=ALU.mult),
                  r=[('Osb', g), 'PCbank'], w=[('Ao', g)])
            S.dma('sp', gin2[2 + hh][:, i * 512:(i + 1) * 512], Ao[g][lo:hi, :], r=[('Ao', g)], w=[('gin2a', hh, i)], ch=('ao', g))
        if after_head is not None:
            after_head(hh)


TC = 256
I32 = mybir.dt.int32
TWO_PI = float(2.0 * np.pi)
GELU_K = float(2.0 * np.sqrt(2.0 / np.pi))


def phase2_ssm_prep(S, A):
    nc = S.nc
    ECS = S.sb("ECS", [128, 8, 2, TC], F32)
    Ec = ECS[:, :, 0, :]
    Es = ECS[:, :, 1, :]
    Bp1 = S.sb("Bp1", [128, 8, 128], BF16)
    Bp2 = S.sb("Bp2", [128, 8, 128], BF16)
    W1p = S.sb("W1p", [128, 8, 128], BF16)
    W2p = S.sb("W2p", [128, 8, 128], BF16)
    R5 = S.sb("R5", [128, 8, 128], F32)
    mag = S.sb("mag", [128, 8], F32)
    carry = S.sb("carry", [128, 8], F32)
    dvec = S.sb("dvec", [128, 1], F32)
    with ExitStack() as esp:
        S.es = esp
        names = ["are", "aim", "ldt", "dt", "th", "c", "s", "cc", "ss", "cs", "abr", "abi", "den", "nr", "fr", "fi",
                 "fin", "frs", "q1", "q2", "pc", "ps", "pss", "kf"]
        T = {n: S.sb("sp_" + n, [128, 8], F32) for n in names}
        ki = S.sb("sp_ki", [128, 8], I32)
        sgn = S.sb("sgn", [128, 2], F32)
        T1 = S.sb("T1", [128, 8, 16], F32)
        T2 = S.sb("T2", [128, 8, 16], F32)
        CA = S.sb("CA", [128, 8, 16], F32)
        CB = S.sb("CB", [128, 8, 16], F32)
        tmp16 = S.sb("tmp16", [128, 16], F32)
        Bw1 = S.sb("Bw1", [128, 8, 128], F32)
        Bw2 = S.sb("Bw2", [128, 8, 128], F32)
        identf = S.sb("identf2", [128, 128], F32)
        jmat = S.sb("jmat", [128, 128], F32)
        tmpR = S.sb("tmpR", [128, 128], F32)
        X1 = S.sb("X1", [128, 8, TC // 2], F32)
        X2 = S.sb("X2", [128, 8, TC // 2], F32)
        PTr = [S.ps(f"PTr{i}", [128, 512], F32) for i in range(2)]
        for nm, src in (("are", "s_are"), ("aim", "s_aim"), ("ldt", "s_ldt")):
            S.dma('sp', T[nm][:], A[src][:, :], w=[nm], ch='misc')
        S.dma('sp', sgn[:], A["sgn"][:, :], w=['sgn'], ch='misc')
        S.dma('sp', dvec[:], A["s_d"][:, :], w=['dvec'], ch='misc')
        S.dma('sp', T1[:], A["s_T1"][:, :, :], w=['T1'], ch='misc')
        S.dma('sp', T2[:], A["s_T2"][:, :, :], w=['T2'], ch='misc')
        S.dma('sp', CA[:], A["s_CA"][:, :, :], w=['CA'], ch='misc')
        S.dma('sp', CB[:], A["s_CB"][:, :, :], w=['CB'], ch='misc')
        S.dma('sp', identf[:], A["ident_f"][:, :], w=['identf2'], ch='misc')
        S.dma('sp', jmat[:], A["jmat"][:, :], w=['jmat'], ch='misc')

        def tt(o, a, b, op, eng='dve'):
            S.add(eng, lambda e: e.tensor_tensor(out=T[o][:], in0=T[a][:], in1=T[b][:], op=op), r=[a, b], w=[o])

        def tsc(o, a, s1, s2, op0, op1=None, eng='dve'):
            if op1 is None:
                S.add(eng, lambda e: e.tensor_scalar(out=T[o][:], in0=T[a][:], scalar1=s1, scalar2=None, op0=op0), r=[a], w=[o])
            else:
                S.add(eng, lambda e: e.tensor_scalar(out=T[o][:], in0=T[a][:], scalar1=s1, scalar2=s2, op0=op0, op1=op1), r=[a], w=[o])

        def act(o, a, func, scale=1.0):
            S.add('act', lambda e: e.activation(out=T[o][:], in_=T[a][:], func=func, scale=scale), r=[a], w=[o])

        act("dt", "ldt", AF.Exp)
        tt("q1", "are", "dt", ALU.mult)
        act("q2", "q1", AF.Exp)
        S.add('dve', lambda e: e.tensor_copy(out=mag[:], in_=T["q2"][:]), r=["q2"], w=["mag"])
        tt("th", "aim", "dt", ALU.mult)
        tsc("kf", "th", 1.0 / TWO_PI, None, ALU.mult)
        S.add('dve', lambda e: e.tensor_copy(out=ki[:], in_=T["kf"][:]), r=["kf"], w=["ki"])
        S.add('dve', lambda e: e.tensor_copy(out=T["kf"][:], in_=ki[:]), r=["ki"], w=["kf"])
        S.add('dve', lambda e: e.scalar_tensor_tensor(out=T["th"][:], in0=T["kf"][:], scalar=-TWO_PI, in1=T["th"][:], op0=ALU.mult, op1=ALU.add), r=["kf", "th"], w=["th"])
        tsc("q1", "th", float(np.pi), -TWO_PI, ALU.is_gt, ALU.mult)
        tt("th", "th", "q1", ALU.add)
        tsc("q1", "th", float(-np.pi), TWO_PI, ALU.is_lt, ALU.mult)
        tt("th", "th", "q1", ALU.add)
        act("s", "th", AF.Sin)
        tsc("q1", "th", -1.0, None, ALU.mult)
        tt("q1", "q1", "th", ALU.max)
        tsc("q1", "q1", -1.0, float(np.pi / 2), ALU.mult, ALU.add)
        act("c", "q1", AF.Sin)
        tt("abr", "q2", "c", ALU.mult)
        tt("abi", "q2", "s", ALU.mult)
        tt("den", "are", "are", ALU.mult)
        tt("q1", "aim", "aim", ALU.mult)
        tt("den", "den", "q1", ALU.add)
        S.add('dve', lambda e: e.reciprocal(out=T["den"][:], in_=T["den"][:]), r=["den"], w=["den"])
        tsc("nr", "abr", -1.0, None, ALU.add)
        tt("q1", "nr", "are", ALU.mult)
        tt("q2", "abi", "aim", ALU.mult)
        tt("fr", "q1", "q2", ALU.add)
        tt("fr", "fr", "den", ALU.mult)
        tt("q1", "abi", "are", ALU.mult)
        tt("q2", "nr", "aim", ALU.mult)
        tt("fi", "q1", "q2", ALU.subtract)
        tt("fi", "fi", "den", ALU.mult)
        S.add('dve', lambda e: e.tensor_scalar(out=T["fin"][:], in0=T["fi"][:], scalar1=sgn[:, 1:2], scalar2=None, op0=ALU.mult), r=["fi", "sgn"], w=["fin"])
        S.add('dve', lambda e: e.tensor_scalar(out=T["frs"][:], in0=T["fr"][:], scalar1=sgn[:, 0:1], scalar2=None, op0=ALU.mult), r=["fr", "sgn"], w=["frs"])
        S.add('pool', lambda e: e.memset(Bw1[:], 0.0), w=["Bw1"])
        S.add('pool', lambda e: e.memset(Bw2[:], 0.0), w=["Bw2"])
        S.add('pool', lambda e: e.memset(W1p[:], 0.0), w=["W1p"])
        S.add('pool', lambda e: e.memset(W2p[:], 0.0), w=["W2p"])
        S.add('pool', lambda e: e.memset(carry[:], 0.0), w=[("carry", gl) for gl in range(8)])
        for gl in range(8):
            cs_ = slice(16 * gl, 16 * gl + 16)
            S.add('dve', lambda e, gl=gl: e.tensor_scalar(out=tmp16[:], in0=T1[:, gl, :], scalar1=T["fr"][:, gl:gl + 1], scalar2=None, op0=ALU.mult), r=["T1", "fr"], w=["tmp16"])
            S.add('dve', lambda e, gl=gl, cs_=cs_: e.scalar_tensor_tensor(out=Bw1[:, gl, cs_], in0=T2[:, gl, :], scalar=T["fin"][:, gl:gl + 1], in1=tmp16[:], op0=ALU.mult, op1=ALU.add), r=["T2", "fin", "tmp16"], w=["Bw1"])
            S.add('dve', lambda e, gl=gl: e.tensor_scalar(out=tmp16[:], in0=T2[:, gl, :], scalar1=T["frs"][:, gl:gl + 1], scalar2=None, op0=ALU.mult), r=["T2", "frs"], w=["tmp16"])
            S.add('dve', lambda e, gl=gl, cs_=cs_: e.scalar_tensor_tensor(out=Bw2[:, gl, cs_], in0=T1[:, gl, :], scalar=T["fi"][:, gl:gl + 1], in1=tmp16[:], op0=ALU.mult, op1=ALU.add), r=["T1", "fi", "tmp16"], w=["Bw2"])
            S.add('dve', lambda e, gl=gl, cs_=cs_: e.tensor_scalar(out=W1p[:, gl, cs_], in0=CA[:, gl, :], scalar1=sgn[:, 0:1], scalar2=None, op0=ALU.mult), r=["CA", "sgn"], w=["W1p"])
            S.add('dve', lambda e, gl=gl, cs_=cs_: e.tensor_scalar(out=W2p[:, gl, cs_], in0=CB[:, gl, :], scalar1=-1.0, scalar2=None, op0=ALU.mult), r=["CB"], w=["W2p"])
        it = 0
        for (Bw, Bp, kw, kp) in ((Bw1, Bp1, "Bw1", "Bp1"), (Bw2, Bp2, "Bw2", "Bp2")):
            for gl in range(8):
                p = it % 2
                it += 1
                S.add('pe', lambda e, Bw=Bw, gl=gl, p=p: e.transpose(out=PTr[p][:, 0:128], in_=Bw[:, gl, :], identity=identf[:]), r=[kw, "identf2"], w=[("PTr", p)])
                S.add('act', lambda e, Bp=Bp, gl=gl, p=p: e.activation(out=Bp[:, gl, :], in_=PTr[p][:, 0:128], func=AF.Copy), r=[("PTr", p)], w=[kp])
        S.add('dve', lambda e: e.tensor_copy(out=Ec[:, :, 0:1], in_=T["c"][:].unsqueeze(2)), r=["c"], w=["Ec"])
        S.add('dve', lambda e: e.tensor_copy(out=Es[:, :, 0:1], in_=T["s"][:].unsqueeze(2)), r=["s"], w=["Es"])
        S.add('dve', lambda e: e.tensor_copy(out=T["pc"][:], in_=T["c"][:]), r=["c"], w=["pc"])
        S.add('dve', lambda e: e.tensor_copy(out=T["ps"][:], in_=T["s"][:]), r=["s"], w=["ps"])
        n = 1
        while n < TC:
            pcb = T["pc"][:].unsqueeze(2).broadcast_to([128, 8, n])
            psb = T["ps"][:].unsqueeze(2).broadcast_to([128, 8, n])
            S.add('dve', lambda e, n=n, pcb=pcb: e.tensor_tensor(out=X1[:, :, 0:n], in0=Ec[:, :, 0:n], in1=pcb, op=ALU.mult), r=["Ec", "pc"], w=["X1"])
            S.add('pool', lambda e, n=n, psb=psb: e.tensor_tensor(out=X2[:, :, 0:n], in0=Es[:, :, 0:n], in1=psb, op=ALU.mult), r=["Es", "ps"], w=["X2"])
            S.add('dve', lambda e, n=n: e.tensor_tensor(out=Ec[:, :, n:2 * n], in0=X1[:, :, 0:n], in1=X2[:, :, 0:n], op=ALU.subtract), r=["X1", "X2"], w=["Ec"])
            S.add('dve', lambda e, n=n, pcb=pcb: e.tensor_tensor(out=X1[:, :, 0:n], in0=Es[:, :, 0:n], in1=pcb, op=ALU.mult), r=["Es", "pc"], w=["X1"])
            S.add('pool', lambda e, n=n, psb=psb: e.tensor_tensor(out=X2[:, :, 0:n], in0=Ec[:, :, 0:n], in1=psb, op=ALU.mult), r=["Ec", "ps"], w=["X2"])
            S.add('dve', lambda e, n=n: e.tensor_tensor(out=Es[:, :, n:2 * n], in0=X1[:, :, 0:n], in1=X2[:, :, 0:n], op=ALU.add), r=["X1", "X2"], w=["Es"])
            tt("cc", "pc", "pc", ALU.mult)
            tt("ss", "ps", "ps", ALU.mult)
            tt("cs", "pc", "ps", ALU.mult)
            tt("pc", "cc", "ss", ALU.subtract)
            tsc("ps", "cs", 2.0, None, ALU.mult)
            n *= 2
        S.add('dve', lambda e: e.tensor_scalar(out=T["pss"][:], in0=T["ps"][:], scalar1=sgn[:, 0:1], scalar2=None, op0=ALU.mult), r=["ps", "sgn"], w=["pss"])
        for gl in range(8):
            S.add('dve', lambda e, gl=gl: e.tensor_scalar(out=tmpR[:], in0=identf[:], scalar1=T["pc"][:, gl:gl + 1], scalar2=None, op0=ALU.mult), r=["identf2", "pc"], w=["tmpR"])
            S.add('dve', lambda e, gl=gl: e.scalar_tensor_tensor(out=R5[:, gl, :], in0=jmat[:], scalar=T["pss"][:, gl:gl + 1], in1=tmpR[:], op0=ALU.mult, op1=ALU.add), r=["jmat", "pss", "tmpR"], w=["R5"])
        S.emit()
    return dict(Ec=Ec, Es=Es, ECS=ECS, Bp1=Bp1, Bp2=Bp2, W1p=W1p, W2p=W2p, R5=R5, mag=mag, carry=carry, dvec=dvec)


def phase2_ssm_main(S, A, uT, gin2, Tn, PCb, lean=False):
    Ec, Es, Bp1, Bp2, W1p, W2p, R5, mag, carry, dvec = (Tn[k] for k in ("Ec", "Es", "Bp1", "Bp2", "W1p", "W2p", "R5", "mag", "carry", "dvec"))
    ECS = Tn["ECS"]
    NS = 6
    t12 = [S.sb(f"st12_{i}", [128, 2 * TC], F32) for i in range(NS)]
    bp = [S.sb(f"sbp{i}", [128, TC], F32) for i in range(NS)]
    z = [S.sb(f"sz{i}", [128, TC], F32) for i in range(NS)]
    zcs = [S.sb(f"szcs{i}", [128, 2, TC], BF16) for i in range(NS)]
    ysb = [S.sb(f"ysb{i}", [128, TC], F32) for i in range(2)]
    yw = [S.sb(f"yw{i}", [128, TC], F32) for i in range(2)]
    yo = [S.sb(f"yo{i}", [128, TC], BF16) for i in range(2)]
    n12, ny, npc = (1, 1, 1) if lean else (3, 2, 2)
    P12l = [S.ps(f"P12_{i}", [128, 512], F32) for i in range(n12)]
    PYl = [S.ps(f"PY{i}", [128, 512], F32) for i in range(ny)]
    PCl = [PCb] + ([S.ps("PCb2", [128, 512], F32)] if npc == 2 else [])
    NB = SEQ // TC
    N = NB * 8

    def info(k):
        tb, gl = divmod(k, 8)
        return tb, gl, slice(tb * TC, (tb + 1) * TC), ('uT', (tb * TC) // 512), tb % ny

    def stage1(k):
        tb, gl, sl, ukey, y = info(k)
        p = k % NS
        P12 = P12l[k % n12]
        k12 = ("P12", k % n12)
        S.add('pe', lambda e: e.matmul(P12[:, 0:TC], lhsT=Bp1[:, gl, :], rhs=uT[:, sl], start=True, stop=True), r=["Bp1", ukey], w=[k12])
        S.add('pe', lambda e: e.matmul(P12[:, TC:2 * TC], lhsT=Bp2[:, gl, :], rhs=uT[:, sl], start=True, stop=True), r=["Bp2", ukey], w=[k12])
        S.add('dve', lambda e: e.tensor_tensor(out=t12[p][:], in0=P12[:, 0:2 * TC], in1=ECS[:, gl, :, :].rearrange("p a b -> p (a b)"), op=ALU.mult), r=[k12, "Ec", "Es"], w=[("st12", p)])

    def stage1b(k):
        p = k % NS
        S.add('dve', lambda e: e.tensor_tensor(out=bp[p][:], in0=t12[p][:, 0:TC], in1=t12[p][:, TC:2 * TC], op=ALU.add), r=[("st12", p)], w=[("sbp", p)])

    def stage2(k):
        tb, gl, sl, ukey, y = info(k)
        p = k % NS
        PCx = PCl[k % npc]
        kpc = ("PCbankS", k % npc)
        PY = PYl[y]
        kpy = ("PY", y)
        S.add('dve', lambda e: e.tensor_tensor_scan(out=z[p][:], data0=mag[:, gl:gl + 1].broadcast_to([128, TC]), data1=bp[p][:], initial=carry[:, gl:gl + 1], op0=ALU.mult, op1=ALU.add),
              r=[("sbp", p), "mag", ("carry", gl)], w=[("sz", p)])
        S.add('pe', lambda e: e.matmul(PCx[:, 0:1], lhsT=R5[:, gl, :], rhs=z[p][:, TC - 1:TC], start=True, stop=True), r=["R5", ("sz", p)], w=[kpc])

    def stage2b(k):
        tb, gl, sl, ukey, y = info(k)
        p = k % NS
        PY = PYl[y]
        kpy = ("PY", y)
        S.add('dve', lambda e: e.tensor_tensor(out=zcs[p][:], in0=z[p][:].unsqueeze(1).broadcast_to([128, 2, TC]), in1=ECS[:, gl, :, :], op=ALU.mult), r=[("sz", p), "Ec", "Es"], w=[("szcs", p)])
        S.add('pe', lambda e: e.matmul(PY[:, 0:TC], lhsT=W1p[:, gl, :], rhs=zcs[p][:, 0, :], start=(gl == 0), stop=False), r=["W1p", ("szcs", p)], w=[kpy])
        S.add('pe', lambda e: e.matmul(PY[:, 0:TC], lhsT=W2p[:, gl, :], rhs=zcs[p][:, 1, :], start=False, stop=(gl == 7)), r=["W2p", ("szcs", p)], w=[kpy])

    def stage3(k):
        tb, gl, sl, ukey, y = info(k)
        PCx = PCl[k % npc]
        kpc = ("PCbankS", k % npc)
        S.add('dve', lambda e: e.tensor_copy(out=carry[:, gl:gl + 1], in_=PCx[:, 0:1]), r=[kpc], w=[("carry", gl)])

    def epilogue(tb):
        _, _, sl, ukey, y = info(tb * 8)
        PY = PYl[y]
        yb = tb % 2
        S.add('dve', lambda e: e.scalar_tensor_tensor(out=ysb[y][:], in0=uT[:, sl], scalar=dvec[:, 0:1], in1=PY[:, 0:TC], op0=ALU.mult, op1=ALU.add), r=[ukey, 'dvec', ("PY", y)], w=[("ysb", y)])
        S.add('act', lambda e: e.activation(out=yw[y][:], in_=ysb[y][:], func=AF.Square), r=[("ysb", y)], w=[("yw", y)])
        S.add('dve', lambda e: e.tensor_scalar(out=yw[y][:], in0=yw[y][:], scalar1=0.044715, scalar2=1.0, op0=ALU.mult, op1=ALU.add), r=[("yw", y)], w=[("yw", y)])
        S.add('dve', lambda e: e.tensor_tensor(out=yw[y][:], in0=yw[y][:], in1=ysb[y][:], op=ALU.mult), r=[("yw", y), ("ysb", y)], w=[("yw", y)])
        S.add('act', lambda e: e.activation(out=yw[y][:], in_=yw[y][:], func=AF.Sigmoid, scale=GELU_K), r=[("yw", y)], w=[("yw", y)])
        S.add('dve', lambda e: e.tensor_tensor(out=yo[y][:], in0=yw[y][:], in1=ysb[y][:], op=ALU.mult), r=[("yw", y), ("ysb", y)], w=[("yo", y)])
        S.dma('sp', gin2[0][:, sl], yo[y][0:64, :], r=[("yo", y)], w=[('gin2y', tb, 0)], ch=('yo', y))
        S.dma('sp', gin2[1][:, sl], yo[y][64:128, :], r=[("yo", y)], w=[('gin2y', tb, 1)], ch=('yo', y))

    lag2 = 1 if lean else 2
    for step in range(N + lag2 + 2):
        if step < N:
            stage1(step)
        if 0 <= step - lag2 < N:
            stage2(step - lag2)
        if step < N:
            stage1b(step)
        if 0 <= step - lag2 - 1 < N:
            stage3(step - lag2 - 1)
        if 0 <= step - lag2 < N:
            stage2b(step - lag2)
            if (step - lag2) % 8 == 7:
                epilogue((step - lag2) // 8)
        yield


def phase3_mix(S, A, gin1, gout2, x1s, x2s, arena, P):
    hT = arena[:, 0:8 * TOK].rearrange("p (k n) -> p k n", k=8)
    stg = [S.sb(f"stg{i}", [128, 2048], F32) for i in range(2)]
    glw = S.sb("glw", [128, 4, 512], BF16)
    wbs = S.sb("wbs", [128, 4, D], BF16)
    wba = S.sb("wba", [128, 4, D], BF16)
    wgt = S.sb("wgt", [128, 8, 2048], BF16)
    wo = S.sb("wo", [128, 8, D], BF16)
    gm = S.sb("gmix3", [128, 8], F32)
    glb = S.sb("glb", [128, 4], F32)
    Yg = S.sb("Yg", [128, 4, 512], BF16)
    At = S.sb("At", [128, 4, 512], BF16)
    ygl = S.sb("ygl", [128, 4, 512], BF16)
    h1b = S.sb("h1b", [128, 8, 512], BF16)
    mT = S.sb("mT", [128, 8, 512], BF16)
    sgz = S.sb("sgz", [128, 512], F32)
    sgs = [S.sb(f"sgs{i}", [128, 512], F32) for i in range(2)]
    sga = [S.sb(f"sga{i}", [128, 512], F32) for i in range(2)]
    m1 = [S.sb(f"m1_{i}", [128, 512], F32) for i in range(2)]
    m2 = [S.sb(f"m2_{i}", [128, 512], F32) for i in range(2)]
    x1t = [S.sb(f"x1t{i}", [128, D], F32) for i in range(2)]
    x2t = [S.sb(f"x2t{i}", [128, D], F32) for i in range(2)]
    PX = [S.ps(f"PX{i}", [128, 512], F32) for i in range(2)]
    PAa = S.ps("PAa", [128, 512], F32)
    PBb = S.ps("PBb", [128, 512], F32)
    PGs = S.ps("PGs", [128, 512], F32)
    PGa = S.ps("PGa", [128, 512], F32)
    S.dma('sp', gm[:], A["mix_gain"][:, :], w=['gmix3'], ch='misc')
    S.dma('sp', glb[:], A["glu_b"][:, :], w=['glb'], ch='misc')
    si = [0]

    def stage(src_ap, ncols, dst_ap, key, gain_col=None, shape3=None):
        sidx = si[0] % 2
        si[0] += 1
        st = stg[sidx][:, 0:ncols]
        stv = st if shape3 is None else st.rearrange("p (a b) -> p a b", a=shape3)
        S.dma('sp', stv, src_ap, w=[('stg', sidx)], ch=('stg', sidx))
        if sidx == 0:
            if gain_col is None:
                S.add('dve', lambda e: e.tensor_copy(out=dst_ap, in_=stv), r=[('stg', sidx)], w=[key])
            else:
                S.add('dve', lambda e: e.tensor_scalar(out=dst_ap, in0=stv, scalar1=gain_col, scalar2=None, op0=ALU.mult), r=[('stg', sidx), 'gmix3'], w=[key])
        else:
            if gain_col is None:
                S.add('act', lambda e: e.activation(out=dst_ap, in_=stv, func=AF.Copy), r=[('stg', sidx)], w=[key])
            else:
                S.add('act', lambda e: e.activation(out=dst_ap, in_=stv, func=AF.Copy, scale=gain_col), r=[('stg', sidx), 'gmix3'], w=[key])

    stage(A["glu_w"].rearrange("(k p) f -> p k f", p=128), 2048, glw[:], 'glw', shape3=4)
    for kk in range(2):
        stage(A["w_bs"].rearrange("(k p) f -> p k f", p=128)[:, 2 * kk:2 * kk + 2, :], 2048, wbs[:, 2 * kk:2 * kk + 2, :], ('wbs', kk), shape3=2)
        stage(A["w_ba"].rearrange("(k p) f -> p k f", p=128)[:, 2 * kk:2 * kk + 2, :], 2048, wba[:, 2 * kk:2 * kk + 2, :], ('wba', kk), shape3=2)
    wiv = A["w_in"].rearrange("(k p) f -> p k f", p=128)
    for k in range(8):
        stage(wiv[:, k, 2048:4096], 2048, wgt[:, k, :], ('wgt', k), gain_col=gm[:, k:k + 1])
    for kk in range(4):
        stage(A["w_out"].rearrange("(k p) f -> p k f", p=128)[:, 2 * kk:2 * kk + 2, :], 2048, wo[:, 2 * kk:2 * kk + 2, :], ('wo', kk), shape3=2)
    wbs_k = [('wbs', 0), ('wbs', 1)]
    wba_k = [('wba', 0), ('wba', 1)]
    wgt_k = [('wgt', k) for k in range(8)]
    wo_k = [('wo', kk) for kk in range(4)]
    g2v = [g.rearrange("(r c) n -> c r n", r=4) for g in gout2]
    g1v = [g.rearrange("(k p) n -> p k n", p=128) for g in gin1]
    px = 0
    pend3 = []
    pidc = {}

    def base_off(e):
        if 'v' not in pidc:
            pidc['v'] = (e.partition_id() % 4) * TOK
        return pidc['v']

    for g in range(4):
        for q in range(4):
            dst = (Yg if q < 2 else At)[64 * (q % 2):64 * (q % 2) + 64, :, :]

            def ld(e, g=g, q=q, dst=dst):
                off = base_off(e) + g * 512
                return e.dma_start(out=dst, in_=g2v[q][:, :, bass.ds(off, 512)])
            S.add('sp', ld, r=[('gout2', q)], w=['Yg' if q < 2 else 'At'], inc=16, semname=('d', 'YgAt'))
        S.dma('sp', h1b[:], g1v[g][:, :, :], r=[('gin1', t) for t in range(4 * g, 4 * g + 4)], w=['h1b'], ch='h1b')
        for m in range(4):
            p = px % 2
            px += 1
            for r_ in range(4):
                S.add('pe', lambda e, m=m, r_=r_, p=p: e.matmul(PX[p][:], lhsT=glw[:, r_, m * 128:(m + 1) * 128], rhs=Yg[:, r_, :], start=(r_ == 0), stop=(r_ == 3)), r=['glw', 'Yg'], w=[('PX', p)])
            S.add('act', lambda e, m=m, p=p: e.activation(out=sgz[:], in_=PX[p][:], func=AF.Sigmoid, bias=glb[:, m:m + 1]), r=[('PX', p), 'glb'], w=['sgz'])
            S.add('dve', lambda e, m=m: e.tensor_tensor(out=ygl[:, m, :], in0=Yg[:, m, :], in1=sgz[:], op=ALU.mult), r=['Yg', 'sgz'], w=[('ygl', m)])
        yk = [('ygl', m) for m in range(4)]
        for dc in range(8):
            q = dc % 2
            dsl = slice(dc * 128, (dc + 1) * 128)
            for m in range(4):
                S.add('pe', lambda e, m=m, dsl=dsl: e.matmul(PAa[:], lhsT=wbs[:, m, dsl], rhs=ygl[:, m, :], start=(m == 0), stop=(m == 3)), r=wbs_k + yk, w=['PAa'])
            for r_ in range(4):
                S.add('pe', lambda e, r_=r_, dsl=dsl: e.matmul(PBb[:], lhsT=wba[:, r_, dsl], rhs=At[:, r_, :], start=(r_ == 0), stop=(r_ == 3)), r=wba_k + ['At'], w=['PBb'])
            for k in range(8):
                S.add('pe', lambda e, k=k, dc=dc: e.matmul(PGs[:], lhsT=wgt[:, k, dc * 128:(dc + 1) * 128], rhs=h1b[:, k, :], start=(k == 0), stop=(k == 7)), r=wgt_k + ['h1b'], w=['PGs'])
            for k in range(8):
                S.add('pe', lambda e, k=k, dc=dc: e.matmul(PGa[:], lhsT=wgt[:, k, 1024 + dc * 128:1024 + (dc + 1) * 128], rhs=h1b[:, k, :], start=(k == 0), stop=(k == 7)), r=wgt_k + ['h1b'], w=['PGa'])
            S.add('act', lambda e, q=q: e.activation(out=sgs[q][:], in_=PGs[:], func=AF.Sigmoid), r=['PGs'], w=[('sgs', q)])
            S.add('act', lambda e, q=q: e.activation(out=sga[q][:], in_=PGa[:], func=AF.Sigmoid), r=['PGa'], w=[('sga', q)])
            S.add('dve', lambda e, q=q: e.tensor_tensor(out=m1[q][:], in0=sgs[q][:], in1=PAa[:], op=ALU.mult), r=[('sgs', q), 'PAa'], w=[('m1', q)])
            S.add('dve', lambda e, q=q: e.tensor_tensor(out=m2[q][:], in0=sga[q][:], in1=PBb[:], op=ALU.mult), r=[('sga', q), 'PBb'], w=[('m2', q)])
            S.add('pool', lambda e, q=q, dc=dc: e.tensor_tensor(out=mT[:, dc, :], in0=m1[q][:], in1=m2[q][:], op=ALU.add), r=[('m1', q), ('m2', q)], w=[('mT', dc)])
        mk = [('mT', dc) for dc in range(8)]
        for tt in range(4):
            t = 4 * g + tt
            s = t % 2
            S.dma('sp', x1t[s][:], x1s[t * 128:(t + 1) * 128, :], r=[('x1s', t)], w=[('x1t', s)], ch=('x1t', s))
            for half in range(2):
                p = px % 2
                px += 1
                for dc in range(8):
                    S.add('pe', lambda e, dc=dc, tt=tt, half=half, p=p: e.matmul(PX[p][:], lhsT=mT[:, dc, tt * 128:(tt + 1) * 128], rhs=wo[:, dc, half * 512:(half + 1) * 512], start=(dc == 0), stop=(dc == 7)),
                          r=mk + wo_k, w=[('PX', p)])
                S.add('dve', lambda e, s=s, half=half, p=p: e.tensor_tensor(out=x2t[s][:, half * 512:(half + 1) * 512], in0=PX[p][:], in1=x1t[s][:, half * 512:(half + 1) * 512], op=ALU.add),
                      r=[('PX', p), ('x1t', s)], w=[('x2t', s)])
            S.dma('pool', x2s[t * 128:(t + 1) * 128, :], x2t[s][:], r=[('x2t', s)], w=[('x2s', t)], ch=('x2st', s))
            if pend3:
                tp, sp_ = pend3.pop()
                norm_T(S, P, x2t[sp_][:], ('x2t', sp_), sp_, hT[:, :, tp * 128:(tp + 1) * 128], ('f2', 'hT', tp))
            pend3.append((t, s))
    tp, sp_ = pend3.pop()
    norm_T(S, P, x2t[sp_][:], ('x2t', sp_), sp_, hT[:, :, tp * 128:(tp + 1) * 128], ('f2', 'hT', tp))


def build_program(dbg=0, stop=99):
    nc = bass.Bass("TRN2", target_bir_lowering=False)
    A = {}

    def inp(name, shape, dt=F32):
        A[name] = nc.dram_tensor(name, list(shape), dt, kind="ExternalInput").ap()
        return A[name]

    x = inp("x", [TOK, D])
    for f in ("ffn1", "ffn2"):
        inp(f + "_wg", [D, FF]); inp(f + "_wu", [D, FF]); inp(f + "_wd", [FF, D]); inp(f + "_gain", [128, 8])
    inp("ident_bf", [128, 128], BF16); inp("ident_f", [128, 128]); inp("jmat", [128, 128]); inp("tri", [128, 128], BF16)
    inp("mix_gain", [128, 8]); inp("w2", [D, 768]); inp("cos", [64, SEQ]); inp("sins", [64, SEQ]); inp("ind", [32, SEQ], BF16)
    inp("fut", [128, 32, 32]); inp("negb", [128, 32, 32]); inp("sgn", [128, 2])
    inp("s_are", [128, 8]); inp("s_aim", [128, 8]); inp("s_ldt", [128, 8]); inp("s_d", [128, 1])
    inp("s_T1", [128, 8, 16]); inp("s_T2", [128, 8, 16]); inp("s_CA", [128, 8, 16]); inp("s_CB", [128, 8, 16])
    inp("glu_w", [512, 512]); inp("glu_b", [128, 4]); inp("w_bs", [512, D]); inp("w_ba", [512, D]); inp("w_in", [D, 4096]); inp("w_out", [D, D])
    inp("fin_gain", [128, D])
    out = nc.dram_tensor("out", [TOK, D], F32, kind="ExternalOutput").ap()
    x1s = nc.dram_tensor("x1s", [TOK, D], F32).ap()
    x2s = nc.dram_tensor("x2s", [TOK, D], F32).ap()
    gin1 = [nc.dram_tensor(f"gin1_{q}", [D, 512], BF16).ap() for q in range(4)]
    gout1 = [nc.dram_tensor(f"gout1_{q}", [4 * D, 512], BF16).ap() for q in range(4)]
    gin2 = [nc.dram_tensor(f"gin2_{q}", [64, SEQ], BF16).ap() for q in range(4)]
    gout2 = [nc.dram_tensor(f"gout2_{q}", [4 * 64, SEQ], BF16).ap() for q in range(4)]
    if dbg:
        dbg_g2 = nc.dram_tensor("dbg_g2", [256, SEQ], BF16, kind="ExternalOutput").ap()
        dbg_x2 = nc.dram_tensor("dbg_x2", [TOK, D], F32, kind="ExternalOutput").ap()
    RG = [[0, 1, 2, 3], [4, 5, 6, 7]]

    with ExitStack() as top:
        S = Sched(nc, top)
        with ExitStack() as es:
            S.es = es
            ident = S.sb("ident", [128, 128], BF16)
            S.dma('sp', ident[:], A["ident_bf"][:, :], w=['ident'], ch='misc')
            g1 = S.sb("g1", [128, 8], F32)
            S.dma('sp', g1[:], A["ffn1_gain"][:, :], w=[('f1', 'gain')], ch='misc')
            arena = S.sb("arena", [128, NJ * D], BF16)
            hT = arena[:, 0:8 * TOK].rearrange("p (k n) -> p k n", k=8)
            P = {**alloc_norm(S, "n1"), 'ident': ident}
            xin = [S.sb(f"xin{i}", [128, D], F32) for i in range(2)]
            for t in range(NT):
                s = t % 2
                S.dma('sp', xin[s][:], x[t * 128:(t + 1) * 128, :], w=[('xin', s)], ch=('xin', s))
                norm_T(S, P, xin[s][:], ('xin', s), s, hT[:, :, t * 128:(t + 1) * 128], ('f1', 'hT', t))
            h1T = [S.sb(f"h1T{i}", [128, 8, 128], BF16) for i in range(2)]
            g1v = [g.rearrange("(k p) n -> p k n", p=128) for g in gin1]

            def epi1(t, xn, xkey, s):
                S.dma('pool', x1s[t * 128:(t + 1) * 128, :], xn[:], r=[xkey], w=[('x1s', t)], ch=('x1st', s))
                norm_T(S, P, xn[:], xkey, s, h1T[s][:], ('h1T', s))
                S.dma('pool', g1v[t // 4][:, :, (t % 4) * 128:(t % 4 + 1) * 128], h1T[s][:], r=[('h1T', s)], w=[('gin1', t)], ch=('g1st', s))
                if t % 4 == 3:
                    q = t // 4
                    S.add('pool', lambda e, q=q: e.collective_compute("AllGather", ALU.bypass, replica_groups=RG, ins=[gin1[q].rearrange("(a b) n -> a (b n)", a=128)], outs=[gout1[q].rearrange("(a b) n -> a (b n)", a=512)]),
                          r=[('gin1', tt) for tt in range(4 * q, 4 * q + 4)], w=[('hG', q)], inc=1, semname='cc1')

            ffn_core(S, arena, xin, lambda t: x[t * 128:(t + 1) * 128, :], A["ffn1_wg"], A["ffn1_wu"], A["ffn1_wd"], g1, epi1, 'f1')
            if stop == 1:
                S.fence('sp', [('hG', q) for q in range(4)])
            S.emit()
        if stop == 1:
            return nc
        with ExitStack() as es2:
            S.es = es2
            uT = S.sb("uT", [128, SEQ], BF16)
            with ExitStack() as esq:
                S.es = esq
                QA = [S.sb(f"QA{i}", [96, SEQ], BF16) for i in range(2)]
                KA = [S.sb(f"KA{i}", [96, SEQ], BF16) for i in range(2)]
                VA = [S.sb(f"VA{i}", [128, 64, 128], BF16) for i in range(2)]
                ksum = [S.sb(f"ksum{i}", [64, 32], F32) for i in range(2)]
                with ExitStack() as esa:
                    S.es = esa
                    phase2_proj(S, A, gout1, uT, QA, KA, VA, ksum)
                    S.emit()
                with ExitStack() as esg:
                    S.es = esg
                    phase2_gate(S, A, QA, ksum)
                    S.emit()
                g2k = [[('gin2y', tb, 0) for tb in range(SEQ // TC)], [('gin2y', tb, 1) for tb in range(SEQ // TC)],
                       [('gin2a', 0, i) for i in range(16)], [('gin2a', 1, i) for i in range(16)]]

                def ag2(q):
                    if dbg:
                        S.dma('sp', dbg_g2[64 * q:64 * q + 64, :], gin2[q][:, :], r=g2k[q], w=[('dbg_g2', q)], ch='dbg')
                    S.add('pool', lambda e, q=q: e.collective_compute("AllGather", ALU.bypass, replica_groups=RG, ins=[gin2[q].rearrange("r (x n) -> (r x) n", x=2)], outs=[gout2[q].rearrange("r (x n) -> (r x) n", x=2)]),
                          r=g2k[q], w=[('gout2', q)], inc=1, semname='cc2')

                if _DBG_NOINTER:
                    with ExitStack() as esc:
                        S.es = esc
                        Tn = phase2_ssm_prep(S, A)
                        S.es = esc
                        PCb = S.ps("PCb", [128, 512], F32)
                        for _ in phase2_ssm_main(S, A, uT, gin2, Tn, PCb):
                            pass
                        S.emit()
                    with ExitStack() as esc:
                        S.es = esc
                        ag2(0)
                        ag2(1)
                        PCa = S.ps("PCa", [128, 512], F32)
                        for _ in phase2_attn(S, A, QA, KA, VA, gin2, PCa, after_head=lambda hh: ag2(2 + hh)):
                            pass
                        if stop == 5:
                            S.fence('sp', [('gout2', q) for q in range(4)] + [('dbg_g2', q) for q in range(4)])
                        S.emit()
                else:
                    with ExitStack() as esc:
                        S.es = esc
                        Tn = phase2_ssm_prep(S, A)
                        S.es = esc
                        PCb = S.ps("PCb", [128, 512], F32)
                        PCa = S.ps("PCa", [128, 512], F32)
                        ga = phase2_attn(S, A, QA, KA, VA, gin2, PCa, after_head=lambda hh: ag2(2 + hh), nps=3)
                        gs = phase2_ssm_main(S, A, uT, gin2, Tn, PCb, lean=True)
                        n_att = sum(4 * i + 4 for i in range(16)) * 2
                        n_ssm = (SEQ // TC) * 8 + 3
                        credit = 0.0
                        ssm_done = False
                        for _ in ga:
                            credit += n_ssm / n_att * 1.3
                            while credit >= 1.0 and not ssm_done:
                                credit -= 1.0
                                try:
                                    next(gs)
                                except StopIteration:
                                    ssm_done = True
                                    ag2(0)
                                    ag2(1)
                        if not ssm_done:
                            for _ in gs:
                                pass
                            ag2(0)
                            ag2(1)
                        if stop == 5:
                            S.fence('sp', [('gout2', q) for q in range(4)] + [('dbg_g2', q) for q in range(4)])
                        S.emit()
            if stop == 5:
                return nc
        with ExitStack() as es3:
            S.es = es3
            ident = S.sb("ident3", [128, 128], BF16)
            S.dma('sp', ident[:], A["ident_bf"][:, :], w=['ident'], ch='misc')
            arena = S.sb("arena3", [128, NJ * D], BF16)
            with ExitStack() as es3a:
                S.es = es3a
                P = {**alloc_norm(S, "n3"), 'ident': ident}
                phase3_mix(S, A, gin1, gout2, x1s, x2s, arena, P)
                if dbg:
                    S.dma('sp', dbg_x2[:, :], x2s[:, :], r=[('x2s', t) for t in range(NT)], w=['dbg_x2'], ch='dbg')
                S.emit()
            with ExitStack() as es3b:
                S.es = es3b
                g2 = S.sb("g2", [128, 8], F32)
                S.dma('sp', g2[:], A["ffn2_gain"][:, :], w=[('f2', 'gain')], ch='misc')
                fg = S.sb("fg", [128, D], F32)
                S.dma('sp', fg[:], A["fin_gain"][:, :], w=['fg'], ch='misc')
                xr = [S.sb(f"xr3_{i}", [128, D], F32) for i in range(2)]
                junk = [S.sb(f"fjunk{i}", [128, D], BF16) for i in range(2)]
                ssq = [S.sb(f"fssq{i}", [128, 1], F32) for i in range(2)]
                ot = [S.sb(f"fot{i}", [128, D], F32) for i in range(2)]

                def epi3(t, xn, xkey, s):
                    kq = ('fssq', s)
                    S.add('act', lambda e: e.activation(out=junk[s][:], in_=xn[:], func=AF.Square, accum_out=ssq[s][:]), r=[xkey], w=[('fjunk', s), kq])
                    S.add('dve', lambda e: e.tensor_scalar(out=ssq[s][:], in0=ssq[s][:], scalar1=1.0 / D, scalar2=EPS, op0=ALU.mult, op1=ALU.add), r=[kq], w=[kq])
                    S.add('act', lambda e: e.activation(out=ssq[s][:], in_=ssq[s][:], func=AF.Sqrt), r=[kq], w=[kq])
                    S.add('dve', lambda e: e.reciprocal(out=ssq[s][:], in_=ssq[s][:]), r=[kq], w=[kq])
                    S.add('dve', lambda e: e.scalar_tensor_tensor(out=ot[s][:], in0=xn[:], scalar=ssq[s][:, 0:1], in1=fg[:], op0=ALU.mult, op1=ALU.mult), r=[xkey, kq, 'fg'], w=[('fot', s)])
                    S.dma('pool', out[t * 128:(t + 1) * 128, :], ot[s][:], r=[('fot', s)], w=[('out', t)], ch=('ost', s))

                ffn_core(S, arena, xr, lambda t: x2s[t * 128:(t + 1) * 128, :], A["ffn2_wg"], A["ffn2_wu"], A["ffn2_wd"], g2, epi3, 'f2')
                fk = [('out', t) for t in range(NT)]
                if dbg:
                    fk += [('dbg_g2', q) for q in range(4)] + ['dbg_x2']
                S.fence('sp', fk)
                S.emit()
    return nc


def _perm_sw():
    return np.concatenate([np.arange(32, 64), np.arange(0, 32)])


def make_in_maps(inputs):
    f32 = np.float32
    bf = ml_dtypes.bfloat16
    x = np.asarray(inputs["x"], f32).reshape(8, TOK, D)
    w_in = np.asarray(inputs["w_in"], f32)[0]

    def gl(g):
        return np.ascontiguousarray(np.asarray(g, f32).reshape(8, 128).T)

    ident = np.eye(128, dtype=f32)
    jmat = np.zeros((128, 128), f32)
    jmat[np.arange(128), (np.arange(128) + 64) % 128] = 1.0
    tri = (np.arange(128)[:, None] <= np.arange(128)[None, :]).astype(f32)
    pos = np.arange(SEQ, dtype=f32)
    inv = (10000.0 ** (-np.arange(0, 64, 2, dtype=f32) / 64.0)).astype(f32)
    ang = (pos[None, :] * inv[:, None]).astype(f32)
    cos = np.cos(ang).astype(f32)
    sin = np.sin(ang).astype(f32)
    COS = np.ascontiguousarray(np.concatenate([cos, cos], 0))
    SINS = np.ascontiguousarray(np.concatenate([-sin, sin], 0))
    ind = (np.arange(SEQ)[None, :] // 256 == np.arange(32)[:, None]).astype(f32).astype(bf)
    fut = np.zeros((128, 32, 32), f32)
    negb = np.full((128, 32, 32), NEG_MASK, f32)
    for cur in range(32):
        fut[:, cur, cur:] = -1e30
        negb[:, cur, cur] = 0.0
    sgn = np.ones((128, 2), f32)
    sgn[64:, 0] = -1.0
    sgn[:64, 1] = -1.0
    common = {
        "ident_bf": ident.astype(bf), "ident_f": ident, "jmat": jmat, "tri": tri.astype(bf), "cos": COS, "sins": SINS, "ind": ind,
        "fut": fut, "negb": negb, "sgn": sgn,
        "mix_gain": gl(inputs["mix_norm"][0]),
        "glu_w": np.asarray(inputs["glu_w"], f32)[0], "glu_b": np.ascontiguousarray(np.asarray(inputs["glu_b"], f32)[0].reshape(4, 128).T),
        "w_bs": np.asarray(inputs["w_branch_ssm"], f32)[0], "w_ba": np.asarray(inputs["w_branch_attn"], f32)[0],
        "w_in": w_in, "w_out": np.asarray(inputs["w_out"], f32)[0],
        "fin_gain": np.ascontiguousarray(np.broadcast_to(np.asarray(inputs["final_norm"], f32)[None, :], (128, D))),
    }
    for f in ("ffn1", "ffn2"):
        common[f + "_wg"] = np.asarray(inputs[f + "_w_gate"], f32)[0]
        common[f + "_wu"] = np.asarray(inputs[f + "_w_up"], f32)[0]
        common[f + "_wd"] = np.asarray(inputs[f + "_w_down"], f32)[0]
        common[f + "_gain"] = gl(inputs[f + "_norm"][0])
    a_re = np.asarray(inputs["ssm_a_re"], f32)[0]; a_im = np.asarray(inputs["ssm_a_im"], f32)[0]
    b_re = np.asarray(inputs["ssm_b_re"], f32)[0]; b_im = np.asarray(inputs["ssm_b_im"], f32)[0]
    c_re = np.asarray(inputs["ssm_c_re"], f32)[0]; c_im = np.asarray(inputs["ssm_c_im"], f32)[0]
    ldt = np.asarray(inputs["ssm_log_dt"], f32)[0]; dd = np.asarray(inputs["ssm_d"], f32)[0]
    sw = _perm_sw()
    maps = []
    for c in range(8):
        cs = c % 4
        m = dict(common)
        m["x"] = x[c]
        ucols = np.arange(cs * 128, cs * 128 + 128)
        qcols = 512 + ucols
        kcols = 1024 + ucols
        vcols = 1536 + ucols
        qsw = np.concatenate([qcols[0:64][sw], qcols[64:128][sw]])
        ksw = np.concatenate([kcols[0:64][sw], kcols[64:128][sw]])
        m["w2"] = np.ascontiguousarray(w_in[:, np.concatenate([ucols, qcols, qsw, kcols, ksw, vcols])])
        gs = slice(cs * 8, cs * 8 + 8)
        m["s_are"] = np.ascontiguousarray(np.concatenate([a_re[gs].T, a_re[gs].T], 0))
        m["s_aim"] = np.ascontiguousarray(np.concatenate([a_im[gs].T, a_im[gs].T], 0))
        m["s_ldt"] = np.ascontiguousarray(np.broadcast_to(ldt[gs][None, :], (128, 8)))
        m["s_d"] = np.ascontiguousarray(dd[cs * 128:(cs + 1) * 128].reshape(128, 1))
        bre = b_re[gs].transpose(1, 0, 2); bim = b_im[gs].transpose(1, 0, 2)
        m["s_T1"] = np.ascontiguousarray(np.concatenate([bre, bim], 0))
        m["s_T2"] = np.ascontiguousarray(np.concatenate([bim, bre], 0))
        cre = c_re[gs].transpose(2, 0, 1); cim = c_im[gs].transpose(2, 0, 1)
        m["s_CA"] = np.ascontiguousarray(np.concatenate([cre, cim], 0))
        m["s_CB"] = np.ascontiguousarray(np.concatenate([cim, cre], 0))
        maps.append(m)
    return maps


_NC_CACHE = {}


def kernel(**inputs):
    if 0 not in _NC_CACHE:
        _NC_CACHE[0] = build_program(0)
    nc = _NC_CACHE[0]
    maps = make_in_maps(inputs)
    res = run_bass_kernel_spmd(nc, maps, core_ids=list(range(8)))
    outs = [np.asarray(res.results[c]["out"], dtype=np.float32) for c in range(8)]
    return np.stack(outs, 0).reshape(2, SEQ, D)
```
